# Optimizing a Trainium2 kernel written in Bass

```python
import jax, jax.numpy as jnp
from jax import lax
import numpy as np

D_MODEL = 2048
BATCH = 2
SEQ = 16384
DEPTH = 4

N_MIXERS = 3
EPS = 1e-6
CHUNK = 128
SGU_WIDTH = D_MODEL
SGU_HEADS = 16
SGU_HEAD_DIM = SGU_WIDTH // SGU_HEADS
CONV_WIDTH = 31
POOL_WINDOWS = (2, 4, 8, 16)
POOL_GROUPS = len(POOL_WINDOWS)
POOL_GROUP_DIM = D_MODEL // POOL_GROUPS
D_FF = 4 * D_MODEL
N_SGU = (DEPTH + 2) // 3
N_CONV = (DEPTH + 1) // 3
N_POOL = DEPTH // 3

kernel_name = "hybrid_sgu_conv_pool_adaln_trunk"


def rmsnorm(x, g):
    xf = x.astype(jnp.float32)
    y = xf * lax.rsqrt(jnp.mean(xf * xf, axis=-1, keepdims=True) + EPS)
    return (y * g.astype(jnp.float32)).astype(x.dtype)


def layernorm(x, g, b):
    xf = x.astype(jnp.float32)
    mu = jnp.mean(xf, axis=-1, keepdims=True)
    var = jnp.mean(jnp.square(xf - mu), axis=-1, keepdims=True)
    y = (xf - mu) * lax.rsqrt(var + EPS) * g.astype(jnp.float32) + b.astype(jnp.float32)
    return y.astype(x.dtype)


def modulate(h, shift, scale):
    return h * (1 + scale[:, None, :]) + shift[:, None, :]


def chunked_sgu_mixer(h, w_in, b_in, v_g, v_b, w_s, b_s, w_out, b_out):
    B, S, _ = h.shape
    z = jax.nn.gelu(h @ w_in + b_in)
    u, v = jnp.split(z, 2, axis=-1)
    v = layernorm(v, v_g, v_b)
    v = v.reshape(B, S // CHUNK, CHUNK, SGU_HEADS, SGU_HEAD_DIM)
    causal = jnp.tril(jnp.ones((CHUNK, CHUNK), dtype=bool))
    w = jnp.where(causal, w_s, 0)
    sv = jnp.einsum('hpq,bnqhd->bnphd', w, v) + b_s.T[None, None, :, :, None]
    gated = u * sv.reshape(B, S, SGU_WIDTH)
    return gated @ w_out + b_out


def conformer_conv_mixer(h, w_pw1, b_pw1, w_dw, b_dw, ln_g, ln_b, w_pw2, b_pw2):
    a = jax.nn.glu(h @ w_pw1 + b_pw1, axis=-1)
    y = lax.conv_general_dilated(
        a, w_dw[:, None, :], window_strides=(1,),
        padding=((CONV_WIDTH - 1, 0),),
        dimension_numbers=('NWC', 'WIO', 'NWC'),
        feature_group_count=D_MODEL) + b_dw
    y = jax.nn.silu(layernorm(y, ln_g, ln_b))
    return y @ w_pw2 + b_pw2


def multiscale_pool_mixer(h, w_pool, b_pool, layer_scale):
    B, S, D = h.shape
    hf = h.astype(jnp.float32)
    cs = jnp.cumsum(hf, axis=1)
    count_base = jnp.arange(1, S + 1, dtype=jnp.float32)[None, :, None]
    outs = []
    for g, win in enumerate(POOL_WINDOWS):
        sl = slice(g * POOL_GROUP_DIM, (g + 1) * POOL_GROUP_DIM)
        c_g = cs[..., sl]
        prev = jnp.pad(c_g, ((0, 0), (win, 0), (0, 0)))[:, :S]
        mean = (c_g - prev) / jnp.minimum(count_base, float(win))
        outs.append(mean - hf[..., sl])
    p = jnp.stack(outs, axis=2).astype(h.dtype)
    y = jnp.einsum('bsgi,gio->bsgo', p, w_pool) + b_pool
    return y.reshape(B, S, D) * layer_scale


def squared_relu_mlp(h, w1, w2):
    return jnp.square(jax.nn.relu(h @ w1)) @ w2


def setup_inputs(seed: int = 0) -> dict:
    key = jax.random.key(seed)
    ks = iter(jax.random.split(key, 32))

    def nrm(shape, scale):
        return jax.random.normal(next(ks), shape, dtype=jnp.float32) * scale

    def gain(shape):
        return 1.0 + nrm(shape, 0.02)

    D = D_MODEL
    return {
        "x": nrm((BATCH, SEQ, D), 1.0),
        "c": nrm((BATCH, D), 1.0),
        "w_ada": nrm((DEPTH, D, 6 * D), 0.5 * D ** -0.5),
        "b_ada": nrm((DEPTH, 6 * D), 0.01),
        "norm_mix_g": gain((DEPTH, D)),
        "norm_ffn_g": gain((DEPTH, D)),
        "sgu_w_in": nrm((N_SGU, D, 2 * SGU_WIDTH), D ** -0.5),
        "sgu_b_in": nrm((N_SGU, 2 * SGU_WIDTH), 0.01),
        "sgu_v_g": gain((N_SGU, SGU_WIDTH)),
        "sgu_v_b": nrm((N_SGU, SGU_WIDTH), 0.01),
        "sgu_w_s": nrm((N_SGU, SGU_HEADS, CHUNK, CHUNK), CHUNK ** -0.5),
        "sgu_b_s": gain((N_SGU, SGU_HEADS, CHUNK)),
        "sgu_w_out": nrm((N_SGU, SGU_WIDTH, D), SGU_WIDTH ** -0.5),
        "sgu_b_out": nrm((N_SGU, D), 0.01),
        "conv_w_pw1": nrm((N_CONV, D, 2 * D), D ** -0.5),
        "conv_b_pw1": nrm((N_CONV, 2 * D), 0.01),
        "conv_w_dw": nrm((N_CONV, CONV_WIDTH, D), CONV_WIDTH ** -0.5),
        "conv_b_dw": nrm((N_CONV, D), 0.01),
        "conv_ln_g": gain((N_CONV, D)),
        "conv_ln_b": nrm((N_CONV, D), 0.01),
        "conv_w_pw2": nrm((N_CONV, D, D), D ** -0.5),
        "conv_b_pw2": nrm((N_CONV, D), 0.01),
        "pool_w": nrm((N_POOL, POOL_GROUPS, POOL_GROUP_DIM, POOL_GROUP_DIM), POOL_GROUP_DIM ** -0.5),
        "pool_b": nrm((N_POOL, POOL_GROUPS, POOL_GROUP_DIM), 0.01),
        "pool_scale": gain((N_POOL, D)),
        "mlp_w1": nrm((DEPTH, D, D_FF), D ** -0.5),
        "mlp_w2": nrm((DEPTH, D_FF, D), D_FF ** -0.5),
        "final_g": gain((D,)),
    }


def reference(x, c, w_ada, b_ada, norm_mix_g, norm_ffn_g,
              sgu_w_in, sgu_b_in, sgu_v_g, sgu_v_b, sgu_w_s, sgu_b_s, sgu_w_out, sgu_b_out,
              conv_w_pw1, conv_b_pw1, conv_w_dw, conv_b_dw, conv_ln_g, conv_ln_b, conv_w_pw2, conv_b_pw2,
              pool_w, pool_b, pool_scale, mlp_w1, mlp_w2, final_g):
    c_act = jax.nn.silu(c)
    for i in range(DEPTH):
        mod = c_act @ w_ada[i] + b_ada[i]
        shift_m, scale_m, gate_m, shift_f, scale_f, gate_f = jnp.split(mod, 6, axis=-1)

        h = modulate(rmsnorm(x, norm_mix_g[i]), shift_m, scale_m)
        kind, j = i % N_MIXERS, i // N_MIXERS
        if kind == 0:
            y = chunked_sgu_mixer(h, sgu_w_in[j], sgu_b_in[j], sgu_v_g[j], sgu_v_b[j],
                                  sgu_w_s[j], sgu_b_s[j], sgu_w_out[j], sgu_b_out[j])
        elif kind == 1:
            y = conformer_conv_mixer(h, conv_w_pw1[j], conv_b_pw1[j], conv_w_dw[j], conv_b_dw[j],
                                     conv_ln_g[j], conv_ln_b[j], conv_w_pw2[j], conv_b_pw2[j])
        else:
            y = multiscale_pool_mixer(h, pool_w[j], pool_b[j], pool_scale[j])
        x = x + gate_m[:, None, :] * y

        h = modulate(rmsnorm(x, norm_ffn_g[i]), shift_f, scale_f)
        x = x + gate_f[:, None, :] * squared_relu_mlp(h, mlp_w1[i], mlp_w2[i])
    return rmsnorm(x, final_g)
```

```python
import numpy as np
import concourse.bass as bass
import concourse.mybir as mybir
from concourse.bass_utils import run_bass_kernel_spmd

F32 = mybir.dt.float32
BF16 = mybir.dt.bfloat16
AF = mybir.ActivationFunctionType
ALU = mybir.AluOpType

D = 2048
KC = 16
DFF = 8192
DEPTH = 4
NCORE = 8
BATCH = 2
SEQ = 16384
HALO = 128
EPS = 1e-6
CONVW = 31
GELU_NATIVE = True


class Sched:
    ENG = ("pe", "act", "dve", "pool", "sp")

    def __init__(self, eng_sems):
        self.sem = eng_sems
        self.count = {e: 0 for e in self.ENG}
        self.prog = {e: [] for e in self.ENG}
        self.waited = {e: {} for e in self.ENG}
        self.lastw = {}
        self.readers = {}
        self.semobj = {("eng", e): s for e, s in eng_sems.items()}
        self.dma_count = {}
        self.n_inst = {e: 0 for e in self.ENG}
        self.n_wait = {e: 0 for e in self.ENG}
        self.meta = {e: [] for e in self.ENG}
        self.phases = []

    def phase(self, label):
        self.phases.append((self.n_inst["pe"], label))

    def simulate(self):
        pc = {e: 0 for e in self.ENG}
        val = {}
        progress = True
        while progress:
            progress = False
            for e in self.ENG:
                lst = self.meta[e]
                while pc[e] < len(lst):
                    kind, sid, v = lst[pc[e]]
                    if kind == "wait":
                        if val.get(sid, 0) < v:
                            break
                    elif kind == "inc":
                        val[sid] = val.get(sid, 0) + v
                    pc[e] += 1
                    progress = True
        stuck = {e: (pc[e], len(self.meta[e]), self.meta[e][pc[e]] if pc[e] < len(self.meta[e]) else None,
                     ) for e in self.ENG if pc[e] < len(self.meta[e])}
        return stuck, val

    def _deps(self, reads, writes):
        need = {}
        for b in reads:
            t = self.lastw.get(b)
            if t is not None and need.get(t[0], 0) < t[1]:
                need[t[0]] = t[1]
        for b in writes:
            t = self.lastw.get(b)
            if t is not None and need.get(t[0], 0) < t[1]:
                need[t[0]] = t[1]
            r = self.readers.get(b)
            if r:
                for sid, v in r.items():
                    if need.get(sid, 0) < v:
                        need[sid] = v
        return need

    def _emit_waits(self, e, need):
        w = self.waited[e]
        for sid, v in need.items():
            if w.get(sid, 0) >= v:
                continue
            if sid == ("eng", e):
                if e == "pe" or v > self.count[e]:
                    continue
            w[sid] = v
            h = self.semobj[sid]
            self.prog[e].append(lambda eng, h=h, v=v: eng.wait_ge(h, v))
            self.meta[e].append(("wait", sid, v))
            self.n_wait[e] += 1

    def _commit(self, tok, reads, writes):
        sid, v = tok
        for b in reads:
            r = self.readers.setdefault(b, {})
            if r.get(sid, 0) < v:
                r[sid] = v
        for b in writes:
            self.lastw[b] = tok
            self.readers[b] = {}

    def op(self, e, fn, reads=(), writes=(), inc=True):
        self._emit_waits(e, self._deps(reads, writes))
        if inc:
            self.count[e] += 1
            tok = (("eng", e), self.count[e])
            h = self.sem[e]
            self.prog[e].append(lambda eng, fn=fn, h=h: fn(eng).then_inc(h, 1))
            self.meta[e].append(("inc", ("eng", e), 1))
        else:
            tok = (("eng", e), self.count[e] + 1)
            self.prog[e].append(lambda eng, fn=fn: fn(eng))
        self.n_inst[e] += 1
        self._commit(tok, reads, writes)
        return tok

    def dma(self, q, sem_id, sem_handle, fn, reads=(), writes=()):
        self.semobj[sem_id] = sem_handle
        self._emit_waits(q, self._deps(reads, writes))
        v = self.dma_count.get(sem_id, 0) + 16
        self.dma_count[sem_id] = v
        tok = (sem_id, v)
        self.prog[q].append(lambda eng, fn=fn, h=sem_handle: fn(eng).then_inc(h, 16))
        self.meta[q].append(("inc", sem_id, 16))
        self.n_inst[q] += 1
        self._commit(tok, reads, writes)
        return tok

    def wait_tokens(self, e, toks):
        need = {}
        for sid, v in toks:
            if need.get(sid, 0) < v:
                need[sid] = v
        self._emit_waits(e, need)

    def replay(self, block):
        m = dict(pe=block.tensor, act=block.scalar, dve=block.vector, pool=block.gpsimd, sp=block.sync)
        for e in self.ENG:
            lst = self.prog[e]

            def body(eng, lst=lst):
                for th in lst:
                    th(eng)
            m[e](body)


def _param_names():
    names = []
    for i in range(DEPTH):
        names += [f"gmix{i}", f"gffn{i}"] + [f"ada{i}_{k}" for k in range(6)]
    for j in range(2):
        names += [f"sgu_bu{j}", f"sgu_bv{j}", f"sgu_vg{j}", f"sgu_vb{j}", f"sgu_bo{j}"]
    names += ["cv_ba", "cv_bb", "cv_bdw", "cv_lg", "cv_lb", "cv_b2", "pl_b", "pl_s", "final_g", "c"]
    return names


PNAMES = _param_names()
PIDX = {n: i for i, n in enumerate(PNAMES)}
NV = len(PNAMES)


def build_program(NT, n_layers=DEPTH, NH=2, HW=512, PW=256, NS=4):
    T = NH * HW
    TB = HALO + T
    TOK = HALO + NT * T
    NCH = PW // 128
    nc = bass.Bass("TRN2", target_bir_lowering=False)

    def din(name, shape):
        return nc.dram_tensor(name, shape, F32, kind="ExternalInput").ap()

    xT = din("xT", [D, TOK])
    yT = nc.dram_tensor("yT", [D, NT * T], F32, kind="ExternalOutput").ap()
    prm_d = din("prm", [128, NV * KC])
    wdw_d = din("wdw", [128, KC * CONVW])
    wsT_d = din("wsT", [2, 128, D])
    bs_d = din("bs", [2, 1, D])
    cst_d = din("cst", [3, 128, 128])
    misc_d = din("misc", [128, 1 + 64])
    w_ada = din("w_ada", [DEPTH, D, 6 * D])
    sgu_w_in = din("sgu_w_in", [2, D, 2 * D])
    sgu_w_out = din("sgu_w_out", [2, D, D])
    conv_w_pw1 = din("conv_w_pw1", [1, D, 2 * D])
    conv_w_pw2 = din("conv_w_pw2", [1, D, D])
    pool_w = din("pool_w", [1, 4, 512, 512])
    mlp_w1 = din("mlp_w1", [DEPTH, D, DFF])
    mlp_w2 = din("mlp_w2", [DEPTH, DFF, D])

    from contextlib import ExitStack
    es = ExitStack()

    def sb(name, shape, dt):
        return es.enter_context(nc.sbuf_tensor(name, shape, dt))

    def sem(name):
        return es.enter_context(nc.semaphore(name))

    X = sb("X", [128, KC, TB], F32)
    A = sb("A", [128, KC, TB], BF16)
    Bb = sb("Bb", [128, KC, TB], BF16)
    W = sb("W", [128, NS, KC * PW], BF16)
    PRM = sb("PRM", [128, NV, KC], F32)
    MOD = sb("MOD", [128, DEPTH, 9, KC], F32)
    WDW = sb("WDW", [128, KC, CONVW], F32)
    MISC = sb("MISC", [128, 65], F32)
    identb = sb("identb", [128, 128], BF16)
    cmaskb = sb("cmaskb", [128, 128], BF16)
    onesb = sb("onesb", [128, 128], BF16)
    cact = sb("cact", [128, KC], BF16)
    epsT = sb("epsT", [128, 1], F32)
    ATAIL = sb("ATAIL", [128, KC, 32], BF16)
    HTAIL = sb("HTAIL", [128, KC, 16], BF16)
    NTMP = 4
    NTMPB = 4
    TF = sb("TF", [128, NTMP, HW + 16], F32)
    TBF = sb("TBF", [128, NTMPB, HW], BF16)
    ST = sb("ST", [128, 4, HW], F32)
    NBANK = 6
    ps = es.enter_context(nc.psum_tensor("ps", [128, NBANK, 512], F32))
    pst = es.enter_context(nc.psum_tensor("pst", [128, 2, 1024], BF16))

    S = Sched(dict(pe=sem("s_pe"), act=sem("s_act"), dve=sem("s_dve"), pool=sem("s_pool"), sp=sem("s_sp")))
    wsem = [sem(f"s_w{i}") for i in range(NS)]
    xsems = [sem(f"s_x{h}") for h in range(NH)]
    osem = [sem(f"s_o{i}") for i in range(NTMP)]
    csem = sem("s_c")

    st = dict(bank=0, slot=0, tf=0, tb=0, pstb=0, ot=0, layer=0)

    reserved = set()

    def next_bank():
        while True:
            b = st["bank"]
            st["bank"] = (b + 1) % NBANK
            if b not in reserved:
                return b

    rs = dict(banks={}, cnt={}, pend=[])

    def rs_begin(sbs_own):
        assert not rs["banks"]
        for (sid, off, w) in sbs_own:
            b = next_bank()
            reserved.add(b)
            rs["banks"][sid] = b
            rs["cnt"][sid] = 0

    def rs_emit_one():
        bank, ti, w, first, last = rs["pend"].pop(0)
        S.op("pe", lambda e: e.matmul(ps[:, bank, 0:w], lhsT=onesb[:, :], rhs=TBF[:, ti, 0:w], start=first, stop=last),
             reads=[("tb", ti), "onesb"], writes=[("ps", bank)], inc=True)

    def rs_feed(m, sid, off, w):
        if sid not in rs["banks"]:
            return
        bank = rs["banks"][sid]
        c = rs["cnt"][sid]
        rs["cnt"][sid] = c + 1
        ti = next_tb()
        S.op("act", lambda e: e.activation(out=TBF[:, ti, 0:w], in_=X[:, m, off:off + w], func=AF.Square),
             reads=[("X", m, sid)], writes=[("tb", ti)])
        rs["pend"].append((bank, ti, w, c == 0, c == KC - 1))
        while len(rs["pend"]) > 2:
            rs_emit_one()

    def rs_take(sid):
        if sid not in rs["banks"]:
            return None
        while rs["pend"]:
            rs_emit_one()
        assert rs["cnt"][sid] == KC, (sid, rs["cnt"][sid])
        b = rs["banks"].pop(sid)
        rs["cnt"].pop(sid)
        return b

    ls = dict(banks={}, cnt={}, pend=[])

    def ls_begin(sbs_own):
        assert not ls["banks"]
        for (sid, off, w) in sbs_own:
            b1 = next_bank()
            reserved.add(b1)
            b2 = next_bank()
            reserved.add(b2)
            ls["banks"][sid] = (b1, b2)
            ls["cnt"][sid] = 0

    def ls_emit_one():
        b1, b2, ti, src_ap, rd, w, first, last = ls["pend"].pop(0)
        S.op("pe", lambda e: e.matmul(ps[:, b1, 0:w], lhsT=onesb[:, :], rhs=src_ap, start=first, stop=last),
             reads=rd + ["onesb"], writes=[("ps", b1)], inc=True)
        S.op("pe", lambda e: e.matmul(ps[:, b2, 0:w], lhsT=onesb[:, :], rhs=TBF[:, ti, 0:w], start=first, stop=last),
             reads=[("tb", ti), "onesb"], writes=[("ps", b2)], inc=True)

    def ls_flush():
        while ls["pend"]:
            ls_emit_one()

    def ls_feed(buf, bname, m, sid, off, w):
        if sid not in ls["banks"]:
            return
        b1, b2 = ls["banks"][sid]
        c = ls["cnt"][sid]
        ls["cnt"][sid] = c + 1
        ti = next_tb()
        S.op("act", lambda e: e.activation(out=TBF[:, ti, 0:w], in_=buf[:, m, off:off + w], func=AF.Square),
             reads=[(bname, m, sid)], writes=[("tb", ti)])
        ls["pend"].append((b1, b2, ti, buf[:, m, off:off + w], [(bname, m, sid)], w, c == 0, c == KC - 1))
        while len(ls["pend"]) > 2:
            ls_emit_one()

    def ls_take(sid):
        if sid not in ls["banks"]:
            return None
        ls_flush()
        assert ls["cnt"][sid] == KC
        ls["cnt"].pop(sid)
        return ls["banks"].pop(sid)

    pinned = set()

    def next_slot():
        while True:
            s = st["slot"]
            st["slot"] = (s + 1) % NS
            if s not in pinned:
                return s

    def next_tf():
        i = st["tf"]
        st["tf"] = (i + 1) % NTMP
        return i

    def next_tb():
        i = st["tb"]
        st["tb"] = (i + 1) % NTMPB
        return i

    def P(name, m=None):
        i = PIDX[name]
        if m is None:
            return PRM[:, i, :]
        return PRM[:, i, m:m + 1]

    Wv = lambda slot: W[:, slot, :].rearrange("p (k n) -> p k n", k=KC)

    S.dma("sp", ("c",), csem, lambda e: e.dma_start(out=PRM[:, :, :], in_=prm_d.rearrange("p (v c) -> p v c", v=NV)),
          writes=["PRM"])
    S.dma("sp", ("c",), csem, lambda e: e.dma_start(out=WDW[:, :, :], in_=wdw_d.rearrange("p (m k) -> p m k", m=KC)),
          writes=["WDW"])
    S.dma("sp", ("c",), csem, lambda e: e.dma_start(out=MISC[:, :], in_=misc_d[:, :]), writes=["MISC"])
    S.dma("pool", ("c",), csem, lambda e: e.dma_start(out=cmaskb[:, :], in_=cst_d[0]), writes=["cmaskb"])
    S.dma("pool", ("c",), csem, lambda e: e.dma_start(out=identb[:, :], in_=cst_d[1]), writes=["identb"])
    ctok = (("c",), S.dma_count[("c",)])
    for k in ("PRM", "WDW", "MISC", "cmaskb", "identb"):
        S.lastw[k] = ctok
    S.op("dve", lambda e: e.memset(onesb[:, :], 1.0), writes=["onesb"])
    S.op("dve", lambda e: e.memset(epsT[:, :], EPS), writes=["epsT"])
    S.op("dve", lambda e: e.memset(ATAIL[:, :, :], 0.0), writes=["ATAIL"])
    S.op("dve", lambda e: e.memset(HTAIL[:, :, :], 0.0), writes=["HTAIL"])
    HM = MISC[:, 0:1]
    INVC = MISC[:, 1:65].rearrange("p (g t) -> p g t", g=4)

    def load_piece(src3):
        slot = next_slot()
        S.dma("pool", ("w", slot), wsem[slot],
              lambda e, slot=slot, src3=src3: e.dma_start(out=Wv(slot), in_=src3),
              writes=[("w", slot)])
        return slot

    def wpiece(wd2, r0, c0):
        return wd2[r0:r0 + D, :].rearrange("(k p) n -> p k n", p=128)[:, :, c0:c0 + PW]

    def mm_group(lhs, rhs, nk, w, rd):
        bank = next_bank()
        for k in range(nk):
            S.op("pe", lambda e, k=k, bank=bank: e.matmul(ps[:, bank, 0:w], lhsT=lhs(k), rhs=rhs(k),
                                                          start=(k == 0), stop=(k == nk - 1)),
                 reads=rd(k), writes=[("ps", bank)], inc=(k == nk - 1))
        return bank

    def stats_sum(src_fn, rd_fn, w, square):
        bank = next_bank()
        for m in range(KC):
            if square:
                ti = next_tb()
                if m % 2 == 0:
                    S.op("act", lambda e, m=m, ti=ti: e.activation(out=TBF[:, ti, 0:w], in_=src_fn(m), func=AF.Square),
                         reads=rd_fn(m), writes=[("tb", ti)])
                else:
                    S.op("dve", lambda e, m=m, ti=ti: e.tensor_tensor(out=TBF[:, ti, 0:w], in0=src_fn(m), in1=src_fn(m),
                                                                      op=ALU.mult),
                         reads=rd_fn(m), writes=[("tb", ti)])
                S.op("pe", lambda e, m=m, ti=ti, bank=bank: e.matmul(ps[:, bank, 0:w], lhsT=onesb[:, :], rhs=TBF[:, ti, 0:w],
                                                                     start=(m == 0), stop=(m == KC - 1)),
                     reads=[("tb", ti), "onesb"], writes=[("ps", bank)], inc=True)
            else:
                S.op("pe", lambda e, m=m, bank=bank: e.matmul(ps[:, bank, 0:w], lhsT=onesb[:, :], rhs=src_fn(m),
                                                              start=(m == 0), stop=(m == KC - 1)),
                     reads=rd_fn(m) + ["onesb"], writes=[("ps", bank)], inc=(m == KC - 1))
        return bank

    def rstd_from(bank_or_ap_fn, w, dst_i, reads):
        S.op("act", lambda e: e.activation(out=ST[:, dst_i, 0:w], in_=bank_or_ap_fn(), func=AF.Ln,
                                           bias=epsT[:, 0:1], scale=1.0 / D),
             reads=reads + ["epsT"], writes=[("st", dst_i)])
        S.op("act", lambda e: e.activation(out=ST[:, dst_i, 0:w], in_=ST[:, dst_i, 0:w], func=AF.Exp, scale=-0.5),
             reads=[("st", dst_i)], writes=[("st", dst_i)])

    def rmsnorm_mod(sbs, gs_fn, sh_fn, dst, dname):
        S.phase("norm")
        while rs["pend"]:
            rs_emit_one()
        for (sid, off, w) in sbs:
            bank = rs_take(sid)
            if bank is None:
                bank = stats_sum(lambda m, off=off, w=w: X[:, m, off:off + w], lambda m, sid=sid: [("X", m, sid)], w, True)
            rstd_from(lambda bank=bank, w=w: ps[:, bank, 0:w], w, 0, [("ps", bank)])
            reserved.discard(bank)
            for m in range(KC):
                ti = next_tf()
                S.op("dve", lambda e, m=m, ti=ti, off=off, w=w: e.scalar_tensor_tensor(
                    out=TF[:, ti, 0:w], in0=X[:, m, off:off + w], scalar=gs_fn(m), in1=ST[:, 0, 0:w],
                    op0=ALU.mult, op1=ALU.mult),
                    reads=[("X", m, sid), ("st", 0), MK()], writes=[("tf", ti)])
                S.op("act", lambda e, m=m, ti=ti, off=off, w=w: e.activation(
                    out=dst[:, m, off:off + w], in_=TF[:, ti, 0:w], func=AF.Identity, bias=sh_fn(m), scale=1.0),
                    reads=[("tf", ti), MK()], writes=[(dname, m, sid)])

    def ln_stats(buf, bname, sid, lo, w, base=0):
        pre = ls_take(sid)
        if pre is None:
            b1 = stats_sum(lambda m: buf[:, m, lo:lo + w], lambda m: [(bname, m, sid)], w, False)
            b2 = stats_sum(lambda m: buf[:, m, lo:lo + w], lambda m: [(bname, m, sid)], w, True)
        else:
            b1, b2 = pre
        t4, t5, t6 = next_tf(), next_tf(), next_tf()
        S.op("act", lambda e: e.activation(out=TF[:, t6, 0:w], in_=ps[:, b1, 0:w], func=AF.Copy, scale=1.0 / D),
             reads=[("ps", b1)], writes=[("tf", t6)])
        S.op("dve", lambda e: e.tensor_tensor(out=TF[:, t4, 0:w], in0=TF[:, t6, 0:w], in1=TF[:, t6, 0:w], op=ALU.mult),
             reads=[("tf", t6)], writes=[("tf", t4)])
        S.op("dve", lambda e: e.scalar_tensor_tensor(out=TF[:, t5, 0:w], in0=ps[:, b2, 0:w], scalar=1.0 / D,
                                                     in1=TF[:, t4, 0:w], op0=ALU.mult, op1=ALU.subtract),
             reads=[("ps", b2), ("tf", t4)], writes=[("tf", t5)])
        S.op("dve", lambda e: e.tensor_scalar(out=TF[:, t5, 0:w], in0=TF[:, t5, 0:w], scalar1=0.0, scalar2=None,
                                              op0=ALU.max),
             reads=[("tf", t5)], writes=[("tf", t5)])
        S.op("act", lambda e: e.activation(out=ST[:, base, 0:w], in_=TF[:, t5, 0:w], func=AF.Ln,
                                           bias=epsT[:, 0:1], scale=1.0),
             reads=[("tf", t5), "epsT"], writes=[("st", base)])
        S.op("act", lambda e: e.activation(out=ST[:, base, 0:w], in_=ST[:, base, 0:w], func=AF.Exp, scale=-0.5),
             reads=[("st", base)], writes=[("st", base)])
        S.op("dve", lambda e: e.scalar_tensor_tensor(out=ST[:, base + 1, 0:w], in0=TF[:, t6, 0:w], scalar=-1.0,
                                                     in1=ST[:, base, 0:w], op0=ALU.mult, op1=ALU.mult),
             reads=[("tf", t6), ("st", base)], writes=[("st", base + 1)])
        reserved.discard(b1)
        reserved.discard(b2)

    def gelu_evac(bank, w, bias_ap, out_ap, wr):
        if GELU_NATIVE:
            S.op("act", lambda e: e.activation(out=out_ap, in_=ps[:, bank, 0:w], func=AF.Gelu_apprx_tanh,
                                               bias=bias_ap, scale=1.0),
                 reads=[("ps", bank), "PRM"], writes=wr)
            return
        t0, t1 = next_tf(), next_tf()
        S.op("act", lambda e: e.activation(out=TF[:, t0, 0:w], in_=ps[:, bank, 0:w], func=AF.Identity,
                                           bias=bias_ap, scale=1.0),
             reads=[("ps", bank), "PRM"], writes=[("tf", t0)])
        S.op("dve", lambda e: e.tensor_tensor(out=TF[:, t1, 0:w], in0=TF[:, t0, 0:w], in1=TF[:, t0, 0:w], op=ALU.mult),
             reads=[("tf", t0)], writes=[("tf", t1)])
        S.op("dve", lambda e: e.tensor_scalar(out=TF[:, t1, 0:w], in0=TF[:, t1, 0:w], scalar1=0.044715, scalar2=1.0,
                                              op0=ALU.mult, op1=ALU.add),
             reads=[("tf", t1)], writes=[("tf", t1)])
        S.op("dve", lambda e: e.tensor_tensor(out=TF[:, t1, 0:w], in0=TF[:, t1, 0:w], in1=TF[:, t0, 0:w], op=ALU.mult),
             reads=[("tf", t1), ("tf", t0)], writes=[("tf", t1)])
        S.op("act", lambda e: e.activation(out=TF[:, t1, 0:w], in_=TF[:, t1, 0:w], func=AF.Sigmoid,
                                           scale=1.5957691216057308),
             reads=[("tf", t1)], writes=[("tf", t1)])
        S.op("dve", lambda e: e.tensor_tensor(out=out_ap, in0=TF[:, t1, 0:w], in1=TF[:, t0, 0:w], op=ALU.mult),
             reads=[("tf", t1), ("tf", t0)], writes=wr)

    def resid_evac(bank, w, bias_ap, gate_ap, m, sid, off):
        ti = next_tf()
        S.op("act", lambda e: e.activation(out=TF[:, ti, 0:w], in_=ps[:, bank, 0:w], func=AF.Identity,
                                           bias=bias_ap, scale=1.0),
             reads=[("ps", bank), "PRM"], writes=[("tf", ti)])
        S.op("dve", lambda e: e.scalar_tensor_tensor(out=X[:, m, off:off + w], in0=TF[:, ti, 0:w], scalar=gate_ap,
                                                     in1=X[:, m, off:off + w], op0=ALU.mult, op1=ALU.add),
             reads=[("tf", ti), ("X", m, sid), MK()], writes=[("X", m, sid)])
        rs_feed(m, sid, off, w)

    def piece_loop(npieces, load_fn, body_fn, sbs, lead=0, nch=None, bg=False):
        nch = NCH if nch is None else nch
        lead = min(lead, npieces) if len(sbs) > 1 else 0
        slots = [load_fn(pc) for pc in range(lead)]
        for (sid, off, w) in sbs:
            for pc in range(lead):
                for cc in range(nch):
                    body_fn(pc, slots[pc], cc, sid, off, w)
        for pc in range(lead, npieces):
            slot = load_fn(pc)
            for cc in range(nch):
                for (sid, off, w) in sbs:
                    body_fn(pc, slot, cc, sid, off, w)
            if bg:
                mod_bg(1)

    def dense_out(wd2, src, sname, sbs, bias_name, gate_fn, lead=0, bg=False):
        def body(pc, slot, cc, sid, off, w):
            m = pc * NCH + cc
            bank = mm_group(lambda k: Wv(slot)[:, k, cc * 128:(cc + 1) * 128],
                            lambda k: src[:, k, off:off + w], KC, w,
                            lambda k: [("w", slot), (sname, k, sid)])
            resid_evac(bank, w, P(bias_name, m), gate_fn(m), m, sid, off)
        rs_begin([sb_ for sb_ in sbs if sb_[0] > 0])
        piece_loop(D // PW, lambda pc: load_piece(wpiece(wd2, 0, pc * PW)), body, sbs, lead=lead, bg=bg)

    S.op("act", lambda e: e.activation(out=cact[:, :], in_=P("c"), func=AF.Silu), reads=["PRM"], writes=["cact"])
    from collections import deque
    mod_jobs = deque()
    mod_done = set()

    def MK():
        return ("MOD", st["layer"])

    def mod_piece(i, pc):
        slot = load_piece(wpiece(w_ada[i], 0, pc * PW))
        bank = next_bank()
        for cc in range(NCH):
            for k in range(KC):
                S.op("pe", lambda e, slot=slot, cc=cc, k=k, bank=bank: e.matmul(
                    ps[:, bank, cc:cc + 1], lhsT=Wv(slot)[:, k, cc * 128:(cc + 1) * 128], rhs=cact[:, k:k + 1],
                    start=(k == 0), stop=(k == KC - 1)),
                    reads=[("w", slot), "cact"], writes=[("ps", bank)], inc=(k == KC - 1))
        s_, c_ = divmod(pc * NCH, KC)
        a0 = PIDX[f"ada{i}_0"]
        S.op("dve", lambda e, i=i, bank=bank, a0=a0, s_=s_, c_=c_: e.tensor_tensor(
            out=MOD[:, i, s_, c_:c_ + NCH], in0=ps[:, bank, 0:NCH], in1=PRM[:, a0 + s_, c_:c_ + NCH], op=ALU.add),
            reads=[("ps", bank), "PRM"], writes=[("MODraw", i, pc)])

    NPC_HALF = 3 * D // PW

    def mod_finish(i, part):
        if part == "m":
            raw = [("MODraw", i, pc) for pc in range(NPC_HALF)]
            S.op("dve", lambda e, i=i: e.scalar_tensor_tensor(out=MOD[:, i, 6, :], in0=MOD[:, i, 1, :], scalar=1.0,
                                                              in1=P(f"gmix{i}"), op0=ALU.add, op1=ALU.mult),
                 reads=raw + ["PRM"], writes=[("MOD", i)])
            if i == 2:
                S.op("dve", lambda e, i=i: e.tensor_tensor(out=MOD[:, i, 8, :], in0=MOD[:, i, 2, :], in1=P("pl_s"),
                                                           op=ALU.mult),
                     reads=raw + ["PRM"], writes=[("MOD", i)])
        else:
            raw = [("MODraw", i, pc) for pc in range(NPC_HALF, 2 * NPC_HALF)]
            S.op("dve", lambda e, i=i: e.scalar_tensor_tensor(out=MOD[:, i, 7, :], in0=MOD[:, i, 4, :], scalar=1.0,
                                                              in1=P(f"gffn{i}"), op0=ALU.add, op1=ALU.mult),
                 reads=raw + ["PRM"], writes=[("MOD", i)])
        mod_done.add((i, part))

    for i in range(n_layers):
        for pc in range(6 * D // PW):
            mod_jobs.append((i, pc))

    def mod_bg(n=1):
        for _ in range(n):
            if not mod_jobs:
                return
            i, pc = mod_jobs.popleft()
            mod_piece(i, pc)

    def mod_drain(i, part):
        lim = NPC_HALF if part == "m" else 2 * NPC_HALF
        while mod_jobs and (mod_jobs[0][0] < i or (mod_jobs[0][0] == i and mod_jobs[0][1] < lim)):
            mod_bg(1)
        if (i, "m") not in mod_done:
            mod_finish(i, "m")
        if part == "f" and (i, "f") not in mod_done:
            mod_finish(i, "f")

    mod_drain(0, "m")

    def Mod(i, s, m):
        return MOD[:, i, s, m:m + 1]

    def ffn(i, sbs, bg=False):
        st["layer"] = i
        S.phase(f"ffn{i}")
        rmsnorm_mod(sbs, lambda m: Mod(i, 7, m), lambda m: Mod(i, 3, m), A, "A")
        for q in range(4):
            S.phase(f"ffn{i} w1 q{q}")

            def body1(pc, slot, cc, sid, off, w):
                hc = pc * NCH + cc
                bank = mm_group(lambda k: Wv(slot)[:, k, cc * 128:(cc + 1) * 128],
                                lambda k: A[:, k, off:off + w], KC, w,
                                lambda k: [("w", slot), ("A", k, sid)])
                ti = next_tf()
                S.op("act", lambda e: e.activation(out=TF[:, ti, 0:w], in_=ps[:, bank, 0:w], func=AF.Relu),
                     reads=[("ps", bank)], writes=[("tf", ti)])
                S.op("dve", lambda e: e.tensor_tensor(out=Bb[:, hc, off:off + w], in0=TF[:, ti, 0:w],
                                                      in1=TF[:, ti, 0:w], op=ALU.mult),
                     reads=[("tf", ti)], writes=[("Bb", hc, sid)])
            piece_loop(D // PW, lambda pc, q=q: load_piece(wpiece(mlp_w1[i], 0, q * D + pc * PW)), body1, sbs,
                       lead=(3 if q == 0 else 0), bg=bg)
            S.phase(f"ffn{i} w2 q{q}")

            def body2(pc, slot, cc, sid, off, w):
                m = pc * NCH + cc
                bank = mm_group(lambda k: Wv(slot)[:, k, cc * 128:(cc + 1) * 128],
                                lambda k: Bb[:, k, off:off + w], KC, w,
                                lambda k: [("w", slot), ("Bb", k, sid)])
                S.op("dve", lambda e: e.scalar_tensor_tensor(
                    out=X[:, m, off:off + w], in0=ps[:, bank, 0:w], scalar=Mod(i, 5, m),
                    in1=X[:, m, off:off + w], op0=ALU.mult, op1=ALU.add),
                    reads=[("ps", bank), ("X", m, sid), MK()], writes=[("X", m, sid)])
                if q == 3:
                    rs_feed(m, sid, off, w)
            if q == 3:
                rs_begin([sb_ for sb_ in sbs if sb_[0] > 0])
            piece_loop(D // PW, lambda pc, q=q: load_piece(wpiece(mlp_w2[i], q * D, pc * PW)), body2, sbs, bg=bg)

    def sgu(i, j, sbs, bg=False):
        st["layer"] = i
        rmsnorm_mod(sbs, lambda m: Mod(i, 6, m), lambda m: Mod(i, 0, m), A, "A")
        S.phase("sgu v")

        def bodyv(pc, slot, cc, sid, off, w):
            m = pc * NCH + cc
            bank = mm_group(lambda k: Wv(slot)[:, k, cc * 128:(cc + 1) * 128],
                            lambda k: A[:, k, off:off + w], KC, w,
                            lambda k: [("w", slot), ("A", k, sid)])
            gelu_evac(bank, w, P(f"sgu_bv{j}", m), Bb[:, m, off:off + w], [("Bb", m, sid)])
            ls_feed(Bb, "Bb", m, sid, off, w)
        ls_begin([sb_ for sb_ in sbs if sb_[0] > 0])
        piece_loop(D // PW, lambda pc: load_piece(wpiece(sgu_w_in[j], 0, D + pc * PW)), bodyv, sbs, lead=3, bg=bg)
        ls_flush()
        S.phase("sgu spatial")
        slot = next_slot()
        pinned.add(slot)
        S.dma("pool", ("w", slot), wsem[slot], lambda e, slot=slot: e.dma_start(out=W[:, slot, 0:D].rearrange("p (h q) -> p h q", h=KC),
                                                                             in_=wsT_d[j].rearrange("p (h q) -> p h q", h=KC)),
              writes=[("w", slot)])
        S.dma("pool", ("w", slot), wsem[slot], lambda e, slot=slot: e.dma_start(out=W[0:1, slot, D:2 * D].rearrange("p (h q) -> p h q", h=KC),
                                                                             in_=bs_d[j].rearrange("p (h q) -> p h q", h=KC)),
              writes=[("w", slot)])
        S.op("dve", lambda e, slot=slot: e.tensor_tensor(
            out=W[:, slot, 0:D].rearrange("p (h q) -> p h q", h=KC),
            in0=W[:, slot, 0:D].rearrange("p (h q) -> p h q", h=KC),
            in1=cmaskb[:, :].unsqueeze(1).broadcast_to([128, KC, 128]), op=ALU.mult),
            reads=[("w", slot), "cmaskb"], writes=[("w", slot)])
        def emit_A(m, sid, off, w, base):
            t1, t2 = next_tf(), next_tf()
            S.op("dve", lambda e: e.scalar_tensor_tensor(
                out=TF[:, t1, 0:w], in0=Bb[:, m, off:off + w], scalar=P(f"sgu_vg{j}", m), in1=ST[:, base, 0:w],
                op0=ALU.mult, op1=ALU.mult),
                reads=[("Bb", m, sid), ("st", base), "PRM"], writes=[("tf", t1)])
            S.op("dve", lambda e: e.scalar_tensor_tensor(
                out=TF[:, t2, 0:w], in0=ST[:, base + 1, 0:w], scalar=P(f"sgu_vg{j}", m), in1=TF[:, t1, 0:w],
                op0=ALU.mult, op1=ALU.add),
                reads=[("st", base + 1), ("tf", t1), "PRM"], writes=[("tf", t2)])
            S.op("act", lambda e: e.activation(
                out=Bb[:, m, off:off + w], in_=TF[:, t2, 0:w], func=AF.Identity, bias=P(f"sgu_vb{j}", m), scale=1.0),
                reads=[("tf", t2), "PRM"], writes=[("Bb", m, sid)])

        def emit_T(m, sid, off, w):
            nn = w // 128
            tp = st["pstb"]
            st["pstb"] = 1 - tp
            for n in range(nn):
                S.op("pe", lambda e, n=n: e.transpose(
                    out=pst[:, tp, n * 128:(n + 1) * 128], in_=Bb[:, m, off + n * 128:off + (n + 1) * 128],
                    identity=identb[:, :]),
                    reads=[("Bb", m, sid), "identb"], writes=[("pst", tp)], inc=(n == nn - 1))
            return tp

        def emit_C(w, tp):
            ti = next_tb()
            S.op("act", lambda e: e.activation(out=TBF[:, ti, 0:w], in_=pst[:, tp, 0:w], func=AF.Copy),
                 reads=[("pst", tp)], writes=[("tb", ti)])
            return ti

        def emit_M(m, sid, off, w, ti):
            nn = w // 128
            bank = next_bank()
            for n in range(nn):
                S.op("pe", lambda e, n=n, sl=slot: e.matmul(
                    ps[:, bank, n * 128:(n + 1) * 128], lhsT=TBF[:, ti, n * 128:(n + 1) * 128],
                    rhs=W[:, sl, m * 128:(m + 1) * 128], start=True, stop=False),
                    reads=[("tb", ti), ("w", slot)], writes=[("ps", bank)], inc=True)
                S.op("pe", lambda e, n=n, sl=slot: e.matmul(
                    ps[:, bank, n * 128:(n + 1) * 128], lhsT=onesb[0:1, :],
                    rhs=W[0:1, sl, D + m * 128:D + (m + 1) * 128], start=False, stop=True),
                    reads=[("w", slot), "onesb"], writes=[("ps", bank)], inc=(n == nn - 1))
            S.op("dve", lambda e: e.tensor_copy(out=Bb[:, m, off:off + w], in_=ps[:, bank, 0:w]),
                 reads=[("ps", bank)], writes=[("Bb", m, sid)])

        def emit_U(m, uslot, cc):
            for (sid, off, w) in sbs:
                bank = mm_group(lambda k: Wv(uslot)[:, k, cc * 128:(cc + 1) * 128],
                                lambda k, off=off, w=w: A[:, k, off:off + w], KC, w,
                                lambda k, sid=sid: [("w", uslot), ("A", k, sid)])
                tu = next_tf()
                gelu_evac(bank, w, P(f"sgu_bu{j}", m), TF[:, tu, 0:w], [("tf", tu)])
                S.op("dve", lambda e, tu=tu, off=off, w=w: e.tensor_tensor(
                    out=Bb[:, m, off:off + w], in0=TF[:, tu, 0:w], in1=Bb[:, m, off:off + w], op=ALU.mult),
                    reads=[("tf", tu), ("Bb", m, sid)], writes=[("Bb", m, sid)])

        own_sbs = [sb_ for sb_ in sbs if sb_[0] > 0]
        halo_sbs = [sb_ for sb_ in sbs if sb_[0] == 0]
        S.phase("sgu LN")
        for (sid, off, w) in halo_sbs:
            ln_stats(Bb, "Bb", sid, off, w, base=0)
            for m in range(KC):
                emit_A(m, sid, off, w, 0)
                tp = emit_T(m, sid, off, w)
                ti = emit_C(w, tp)
                emit_M(m, sid, off, w, ti)
        bases = {}
        for idx, (sid, off, w) in enumerate(own_sbs):
            ln_stats(Bb, "Bb", sid, off, w, base=2 * idx)
            bases[sid] = 2 * idx
        S.phase("sgu pipe")
        uslots = {}

        def uslot_of(pc):
            if pc not in uslots:
                uslots[pc] = load_piece(wpiece(sgu_w_in[j], 0, pc * PW))
            return uslots[pc]
        for pc in range(min(2, D // PW)):
            uslot_of(pc)
        prevC = []
        for it in range(KC + 2):
            m = it
            if m < KC:
                for (sid, off, w) in own_sbs:
                    emit_A(m, sid, off, w, bases[sid])
            if it >= 2:
                mu = it - 2
                emit_U(mu, uslot_of(mu // NCH), mu % NCH)
            curC = []
            if m < KC:
                tps = [(sb_, emit_T(m, *sb_)) for sb_ in own_sbs]
                curC = [(sb_, emit_C(sb_[2], tp)) for (sb_, tp) in tps]
            for (sb_, ti) in prevC:
                emit_M(m - 1, sb_[0], sb_[1], sb_[2], ti)
            prevC = curC
        pinned.discard(slot)
        S.phase("sgu out")
        dense_out(sgu_w_out[j], Bb, "Bb", sbs, f"sgu_bo{j}", lambda m: Mod(i, 2, m), bg=bg)

    def conv(i, sbs, first_tile):
        st["layer"] = i
        rmsnorm_mod(sbs, lambda m: Mod(i, 6, m), lambda m: Mod(i, 0, m), A, "A")
        w1v = conv_w_pw1[0].rearrange("(k p) (two n) -> p k two n", p=128, two=2)
        S.phase("conv pw1")
        for m in range(KC):
            slot = next_slot()
            for hf in range(2):
                S.dma("pool", ("w", slot), wsem[slot],
                      lambda e, slot=slot, m=m, hf=hf: e.dma_start(
                          out=W[:, slot, 0:KC * 256].rearrange("p (k two n) -> p k two n", k=KC, two=2)[:, :, hf, :],
                          in_=w1v[:, :, hf, m * 128:(m + 1) * 128]),
                      writes=[("w", slot)])
            wv4 = lambda slot: W[:, slot, 0:KC * 256].rearrange("p (k two n) -> p k two n", k=KC, two=2)
            for (sid, off, w) in sbs:
                bankA = mm_group(lambda k, slot=slot: wv4(slot)[:, k, 0, :],
                                 lambda k, off=off, w=w: A[:, k, off:off + w], KC, w,
                                 lambda k, slot=slot, sid=sid: [("w", slot), ("A", k, sid)])
                bankB = mm_group(lambda k, slot=slot: wv4(slot)[:, k, 1, :],
                                 lambda k, off=off, w=w: A[:, k, off:off + w], KC, w,
                                 lambda k, slot=slot, sid=sid: [("w", slot), ("A", k, sid)])
                ti = next_tf()
                S.op("act", lambda e, m=m, ti=ti, bankB=bankB, w=w: e.activation(
                    out=TF[:, ti, 0:w], in_=ps[:, bankB, 0:w], func=AF.Sigmoid, bias=P("cv_bb", m), scale=1.0),
                    reads=[("ps", bankB), "PRM"], writes=[("tf", ti)])
                S.op("dve", lambda e, m=m, ti=ti, bankA=bankA, off=off, w=w: e.scalar_tensor_tensor(
                    out=Bb[:, m, off:off + w], in0=ps[:, bankA, 0:w], scalar=P("cv_ba", m), in1=TF[:, ti, 0:w],
                    op0=ALU.add, op1=ALU.mult),
                    reads=[("ps", bankA), ("tf", ti), "PRM"], writes=[("Bb", m, sid)])
        allm0 = [("Bb", m, 0) for m in range(KC)]
        lastsid = sbs[-1][0]
        if first_tile:
            S.op("dve", lambda e: e.tensor_scalar(out=Bb[:, :, 0:HALO], in0=Bb[:, :, 0:HALO], scalar1=HM, scalar2=None,
                                                  op0=ALU.mult),
                 reads=allm0 + ["MISC"], writes=allm0)
        else:
            S.op("dve", lambda e: e.tensor_copy(out=Bb[:, :, HALO - 32:HALO], in_=ATAIL[:, :, :]),
                 reads=["ATAIL"], writes=allm0)
        S.op("dve", lambda e: e.tensor_copy(out=ATAIL[:, :, :], in_=Bb[:, :, TB - 32:TB]),
             reads=[("Bb", m, lastsid) for m in range(KC)], writes=["ATAIL"])
        ls_begin([sb_ for sb_ in sbs if sb_[0] > 0])
        S.phase("conv dw")
        for m in range(KC):
            slot = next_slot()
            dv = lambda slot: W[:, slot, 0:CONVW * 128].rearrange("p (k n) -> p k n", k=CONVW)
            S.op("dve", lambda e, slot=slot, m=m: e.tensor_tensor(
                out=dv(slot), in0=identb[:, :].unsqueeze(1).broadcast_to([128, CONVW, 128]),
                in1=WDW[:, m, :].unsqueeze(2).broadcast_to([128, CONVW, 128]), op=ALU.mult),
                reads=["identb", "WDW"], writes=[("w", slot)])
            for (sid, off, w) in sbs:
                lo, ww = (32, 96) if sid == 0 else (off, w)
                bank = next_bank()
                rd = [("w", slot), ("Bb", m, sid)] + ([("Bb", m, sid - 1)] if sid > 0 else [])
                for k in range(CONVW):
                    S.op("pe", lambda e, slot=slot, m=m, k=k, bank=bank, lo=lo, ww=ww: e.matmul(
                        ps[:, bank, 0:ww], lhsT=dv(slot)[:, k, :], rhs=Bb[:, m, lo - 30 + k:lo - 30 + k + ww],
                        start=(k == 0), stop=(k == CONVW - 1)),
                        reads=rd, writes=[("ps", bank)], inc=(k == CONVW - 1))
                S.op("act", lambda e, m=m, bank=bank, lo=lo, ww=ww: e.activation(
                    out=A[:, m, lo:lo + ww], in_=ps[:, bank, 0:ww], func=AF.Identity, bias=P("cv_bdw", m), scale=1.0),
                    reads=[("ps", bank), "PRM"], writes=[("A", m, sid)])
                if sid == 0:
                    S.op("dve", lambda e, m=m: e.memset(A[:, m, 0:32], 0.0), reads=[("A", m, 0)], writes=[("A", m, 0)])
                else:
                    ls_feed(A, "A", m, sid, off, w)
        ls_flush()
        S.phase("conv LN")
        for (sid, off, w) in sbs:
            ln_stats(A, "A", sid, off, w)
            for m in range(KC):
                t1, t2 = next_tf(), next_tf()
                S.op("dve", lambda e, m=m, t1=t1, off=off, w=w: e.tensor_tensor(
                    out=TF[:, t1, 0:w], in0=A[:, m, off:off + w], in1=ST[:, 0, 0:w], op=ALU.mult),
                    reads=[("A", m, sid), ("st", 0)], writes=[("tf", t1)])
                S.op("dve", lambda e, t1=t1, t2=t2, w=w: e.tensor_tensor(
                    out=TF[:, t2, 0:w], in0=TF[:, t1, 0:w], in1=ST[:, 1, 0:w], op=ALU.add),
                    reads=[("tf", t1), ("st", 1)], writes=[("tf", t2)])
                S.op("act", lambda e, m=m, t2=t2, off=off, w=w: e.activation(
                    out=Bb[:, m, off:off + w], in_=TF[:, t2, 0:w], func=AF.Silu, bias=P("cv_lb", m), scale=P("cv_lg", m)),
                    reads=[("tf", t2), "PRM"], writes=[("Bb", m, sid)])
        S.phase("conv pw2")
        dense_out(conv_w_pw2[0], Bb, "Bb", sbs, "cv_b2", lambda m: Mod(i, 2, m), lead=3)

    def pool(i, sbs_all, sbs_own, first_tile):
        st["layer"] = i
        rmsnorm_mod(sbs_all, lambda m: Mod(i, 6, m), lambda m: Mod(i, 0, m), A, "A")
        allm0 = [("A", m, 0) for m in range(KC)]
        lastsid = sbs_own[-1][0]
        if first_tile:
            S.op("dve", lambda e: e.tensor_scalar(out=A[:, :, 0:HALO], in0=A[:, :, 0:HALO], scalar1=HM, scalar2=None,
                                                  op0=ALU.mult),
                 reads=allm0 + ["MISC"], writes=allm0)
        else:
            S.op("dve", lambda e: e.tensor_copy(out=A[:, :, HALO - 16:HALO], in_=HTAIL[:, :, :]),
                 reads=["HTAIL"], writes=allm0)
        S.op("dve", lambda e: e.tensor_copy(out=HTAIL[:, :, :], in_=A[:, :, TB - 16:TB]),
             reads=[("A", m, lastsid) for m in range(KC)], writes=["HTAIL"])
        S.phase("pool win")

        def win_chunk(m):
            g = m // 4
            win = 2 << g
            for (sid, off, w) in sbs_own:
                rdA = [("A", m, sid), ("A", m, sid - 1)]
                b = off - (win - 1)
                a = 1
                src = None
                while a < win:
                    b2 = b + a
                    ln = off + w - b2
                    pi = next_tf()
                    if src is None:
                        S.op("dve", lambda e, m=m, b=b, a=a, ln=ln, pi=pi: e.tensor_tensor(
                            out=TF[:, pi, 0:ln], in0=A[:, m, b + a:b + a + ln], in1=A[:, m, b:b + ln], op=ALU.add),
                            reads=rdA, writes=[("tf", pi)])
                    else:
                        S.op("dve", lambda e, a=a, ln=ln, pi=pi, src=src: e.tensor_tensor(
                            out=TF[:, pi, 0:ln], in0=TF[:, src, a:a + ln], in1=TF[:, src, 0:ln], op=ALU.add),
                            reads=[("tf", src)], writes=[("tf", pi)])
                    src = pi
                    b = b2
                    a *= 2
                S.op("dve", lambda e, m=m, src=src, off=off, w=w, win=win: e.scalar_tensor_tensor(
                    out=Bb[:, m, off:off + w], in0=TF[:, src, 0:w], scalar=1.0 / win, in1=A[:, m, off:off + w],
                    op0=ALU.mult, op1=ALU.subtract),
                    reads=[("tf", src), ("A", m, sid)], writes=[("Bb", m, sid)])
                if first_tile and sid == 1:
                    ti = next_tf()
                    S.op("dve", lambda e, src=src, ti=ti, g=g: e.tensor_tensor(
                        out=TF[:, ti, 0:16], in0=TF[:, src, 0:16], in1=INVC[:, g, :], op=ALU.mult),
                        reads=[("tf", src), "MISC"], writes=[("tf", ti)])
                    S.op("dve", lambda e, m=m, ti=ti, off=off: e.tensor_tensor(
                        out=Bb[:, m, off:off + 16], in0=TF[:, ti, 0:16], in1=A[:, m, off:off + 16], op=ALU.subtract),
                        reads=[("tf", ti), ("A", m, sid)], writes=[("Bb", m, sid)])
        pwv = pool_w[0].rearrange("g (k p) n -> p (g k) n", p=128)
        S.phase("pool mm")
        slots = [load_piece(pwv[:, :, pc * PW:(pc + 1) * PW]) for pc in range(512 // PW)]
        rs_begin(sbs_own)
        for g in range(4):
            for mm_ in range(4):
                win_chunk(g * 4 + mm_)
            for pc in range(512 // PW):
                slot = slots[pc]
                for cc in range(NCH):
                    m = g * 4 + pc * NCH + cc
                    for (sid, off, w) in sbs_own:
                        bank = mm_group(lambda k, slot=slot, cc=cc, g=g: Wv(slot)[:, g * 4 + k, cc * 128:(cc + 1) * 128],
                                        lambda k, off=off, w=w, g=g: Bb[:, g * 4 + k, off:off + w], 4, w,
                                        lambda k, slot=slot, sid=sid, g=g: [("w", slot), ("Bb", g * 4 + k, sid)])
                        resid_evac(bank, w, P("pl_b", m), Mod(i, 8, m), m, sid, off)

    otoks = []

    xkeys = {}

    def load_x_chunk(t, h, m):
        blo = HALO + h * HW
        bhi = blo + HW
        sids = [1 + h]
        if t == 0 and h == 0:
            blo = 0
            sids = [0, 1]
        gc0 = t * T + blo
        ks = [("X", m, s_) for s_ in sids]
        xkeys.setdefault((t, h), []).extend(ks)
        S.dma("sp", ("x", h), xsems[h],
              lambda e: e.dma_start(out=X[:, m, blo:bhi], in_=xT[m * 128:(m + 1) * 128, gc0:gc0 + bhi - blo]),
              writes=ks)

    def finish_load_group(t, h):
        xtok = (("x", h), S.dma_count[("x", h)])
        for k in xkeys.pop((t, h)):
            S.lastw[k] = xtok

    def final_norm_store(t, sbs_own, load_next):
        S.phase("final")
        while rs["pend"]:
            rs_emit_one()
        for (sid, off, w) in sbs_own:
            bank = rs_take(sid)
            if bank is None:
                bank = stats_sum(lambda m, off=off, w=w: X[:, m, off:off + w], lambda m, sid=sid: [("X", m, sid)], w, True)
            rstd_from(lambda bank=bank, w=w: ps[:, bank, 0:w], w, 0, [("ps", bank)])
            reserved.discard(bank)
            for m in range(KC):
                oi = next_tf()
                S.op("dve", lambda e, m=m, oi=oi, off=off, w=w: e.scalar_tensor_tensor(
                    out=TF[:, oi, 0:w], in0=X[:, m, off:off + w], scalar=P("final_g", m), in1=ST[:, 0, 0:w],
                    op0=ALU.mult, op1=ALU.mult),
                    reads=[("X", m, sid), ("st", 0), "PRM"], writes=[("tf", oi)])
                c0 = t * T + off - HALO
                tok = S.dma("sp", ("o", oi), osem[oi],
                            lambda e, m=m, oi=oi, c0=c0, w=w: e.dma_start(out=yT[m * 128:(m + 1) * 128, c0:c0 + w],
                                                                         in_=TF[:, oi, 0:w]),
                            reads=[("tf", oi)])
                otoks.append(tok)
                if load_next:
                    load_x_chunk(t + 1, sid - 1, m)
            if load_next:
                finish_load_group(t + 1, sid - 1)

    for t in range(NT):
        S.phase(f"=== tile {t}")
        first = (t == 0)
        own = [(1 + h, HALO + h * HW, HW) for h in range(NH)]
        withh = [(0, 0, HALO)] + own
        if first:
            for h in range(NH):
                for m in range(KC):
                    load_x_chunk(0, h, m)
                finish_load_group(0, h)
        for i in range(n_layers):
            kind = i % 3
            mod_drain(i, "m")
            if kind == 0:
                sbs = withh if (first and i == 0) else own
                sgu(i, i // 3, sbs, bg=first)
                mod_drain(i, "f")
                ffn(i, sbs, bg=first)
            elif kind == 1:
                sbs = withh if first else own
                conv(i, sbs, first)
                mod_drain(i, "f")
                ffn(i, sbs, bg=first)
            else:
                pool(i, withh if first else own, own, first)
                mod_drain(i, "f")
                ffn(i, own, bg=first)
        final_norm_store(t, own, t + 1 < NT)

    S.wait_tokens("sp", [(("o", oi), S.dma_count.get(("o", oi), 0)) for oi in range(NTMP) if ("o", oi) in S.dma_count])
    block = es.enter_context(nc.Block())
    S.replay(block)
    es.close()
    return nc, S


def _fm(vec):
    return np.ascontiguousarray(np.asarray(vec, np.float32).reshape(KC, 128).T)


def prepare_inputs(inp, NT, T, ncore=NCORE, seq=SEQ):
    x = np.asarray(inp["x"], np.float32)
    own = NT * T
    per_b = seq // own
    vecs = {}
    for i in range(DEPTH):
        vecs[f"gmix{i}"] = inp["norm_mix_g"][i]
        vecs[f"gffn{i}"] = inp["norm_ffn_g"][i]
        for k in range(6):
            vecs[f"ada{i}_{k}"] = inp["b_ada"][i][k * D:(k + 1) * D]
    for j in range(2):
        vecs[f"sgu_bu{j}"] = inp["sgu_b_in"][j][:D]
        vecs[f"sgu_bv{j}"] = inp["sgu_b_in"][j][D:]
        vecs[f"sgu_vg{j}"] = inp["sgu_v_g"][j]
        vecs[f"sgu_vb{j}"] = inp["sgu_v_b"][j]
        vecs[f"sgu_bo{j}"] = inp["sgu_b_out"][j]
    vecs["cv_ba"] = inp["conv_b_pw1"][0][:D]
    vecs["cv_bb"] = inp["conv_b_pw1"][0][D:]
    vecs["cv_bdw"] = inp["conv_b_dw"][0]
    vecs["cv_lg"] = inp["conv_ln_g"][0]
    vecs["cv_lb"] = inp["conv_ln_b"][0]
    vecs["cv_b2"] = inp["conv_b_pw2"][0]
    vecs["pl_b"] = np.asarray(inp["pool_b"][0]).reshape(D)
    vecs["pl_s"] = inp["pool_scale"][0]
    vecs["final_g"] = inp["final_g"]
    wdw = np.ascontiguousarray(
        np.asarray(inp["conv_w_dw"][0], np.float32).T.reshape(KC, 128, CONVW).transpose(1, 0, 2)).reshape(128, KC * CONVW)
    wsT = np.ascontiguousarray(np.asarray(inp["sgu_w_s"], np.float32).transpose(0, 3, 1, 2)).reshape(2, 128, D)
    bs = np.ascontiguousarray(np.asarray(inp["sgu_b_s"], np.float32)).reshape(2, 1, D)
    cst = np.zeros((3, 128, 128), np.float32)
    cst[0] = np.triu(np.ones((128, 128), np.float32))
    cst[1] = np.eye(128, dtype=np.float32)
    shared = dict(wdw=wdw, wsT=wsT, bs=bs, cst=cst)
    for k in ("w_ada", "sgu_w_in", "sgu_w_out", "conv_w_pw1", "conv_w_pw2", "pool_w", "mlp_w1", "mlp_w2"):
        shared[k] = np.ascontiguousarray(np.asarray(inp[k], np.float32))
    in_maps = []
    for r in range(ncore):
        b, s0 = r // per_b, (r % per_b) * own
        xs = np.zeros((HALO + own, D), np.float32)
        if s0 > 0:
            xs[:HALO] = x[b, s0 - HALO:s0]
        xs[HALO:] = x[b, s0:s0 + own]
        v = dict(vecs)
        v["c"] = inp["c"][b]
        prm = np.stack([_fm(v[n]) for n in PNAMES], axis=1).reshape(128, NV * KC)
        misc = np.zeros((128, 65), np.float32)
        misc[:, 0] = 1.0 if s0 > 0 else 0.0
        for g in range(4):
            win = 2 << g
            for tt in range(16):
                cnt = min(tt + 1, win) if s0 == 0 else win
                misc[:, 1 + g * 16 + tt] = np.float32(1.0) / np.float32(cnt)
        m = dict(shared)
        m["xT"] = np.ascontiguousarray(xs.T)
        m["prm"] = np.ascontiguousarray(prm)
        m["misc"] = misc
        in_maps.append(m)
    return in_maps


_CACHE = {}


def run(inp, NT, n_layers=DEPTH, NH=2, ncore=NCORE, seq=SEQ, trace=False):
    T = NH * 512
    key = (NT, n_layers, NH)
    if key not in _CACHE:
        _CACHE[key] = build_program(NT, n_layers=n_layers, NH=NH)[0]
    nc = _CACHE[key]
    in_maps = prepare_inputs(inp, NT, T, ncore=ncore, seq=seq)
    res = run_bass_kernel_spmd(nc, in_maps, core_ids=list(range(ncore)), **({"trace": True} if trace else {}))
    own = NT * T
    per_b = seq // own
    out = np.empty((BATCH, seq, D), np.float32)
    for r in range(ncore):
        b, s0 = r // per_b, (r % per_b) * own
        out[b, s0:s0 + own, :] = res.results[r]["yT"].T
    return out, res


def kernel(**inputs):
    out, _ = run(inputs, NT=4, n_layers=DEPTH, NH=2)
    return out
```

```python
import numpy as np
import concourse.bass as bass
import concourse.mybir as mybir
from concourse.bass_utils import run_bass_kernel_spmd

F32 = mybir.dt.float32
BF16 = mybir.dt.bfloat16
AF = mybir.ActivationFunctionType
ALU = mybir.AluOpType

D = 2048
KC = 16
DFF = 8192
DEPTH = 4
NCORE = 8
BATCH = 2
SEQ = 16384
HALO = 128
EPS = 1e-6
CONVW = 31
GELU_NATIVE = True


class Sched:
    ENG = ("pe", "act", "dve", "pool", "sp")

    def __init__(self, eng_sems):
        self.sem = eng_sems
        self.count = {e: 0 for e in self.ENG}
        self.prog = {e: [] for e in self.ENG}
        self.waited = {e: {} for e in self.ENG}
        self.lastw = {}
        self.readers = {}
        self.semobj = {("eng", e): s for e, s in eng_sems.items()}
        self.dma_count = {}
        self.n_inst = {e: 0 for e in self.ENG}
        self.n_wait = {e: 0 for e in self.ENG}
        self.meta = {e: [] for e in self.ENG}
        self.phases = []

    def phase(self, label):
        self.phases.append((self.n_inst["pe"], label))

    def simulate(self):
        pc = {e: 0 for e in self.ENG}
        val = {}
        progress = True
        while progress:
            progress = False
            for e in self.ENG:
                lst = self.meta[e]
                while pc[e] < len(lst):
                    kind, sid, v = lst[pc[e]]
                    if kind == "wait":
                        if val.get(sid, 0) < v:
                            break
                    elif kind == "inc":
                        val[sid] = val.get(sid, 0) + v
                    pc[e] += 1
                    progress = True
        stuck = {e: (pc[e], len(self.meta[e]), self.meta[e][pc[e]] if pc[e] < len(self.meta[e]) else None,
                     ) for e in self.ENG if pc[e] < len(self.meta[e])}
        return stuck, val

    def _deps(self, reads, writes):
        need = {}
        for b in reads:
            t = self.lastw.get(b)
            if t is not None and need.get(t[0], 0) < t[1]:
                need[t[0]] = t[1]
        for b in writes:
            t = self.lastw.get(b)
            if t is not None and need.get(t[0], 0) < t[1]:
                need[t[0]] = t[1]
            r = self.readers.get(b)
            if r:
                for sid, v in r.items():
                    if need.get(sid, 0) < v:
                        need[sid] = v
        return need

    def _emit_waits(self, e, need):
        w = self.waited[e]
        for sid, v in need.items():
            if w.get(sid, 0) >= v:
                continue
            if sid == ("eng", e):
                if e == "pe" or v > self.count[e]:
                    continue
            w[sid] = v
            h = self.semobj[sid]
            self.prog[e].append(lambda eng, h=h, v=v: eng.wait_ge(h, v))
            self.meta[e].append(("wait", sid, v))
            self.n_wait[e] += 1

    def _commit(self, tok, reads, writes):
        sid, v = tok
        for b in reads:
            r = self.readers.setdefault(b, {})
            if r.get(sid, 0) < v:
                r[sid] = v
        for b in writes:
            self.lastw[b] = tok
            self.readers[b] = {}

    def op(self, e, fn, reads=(), writes=(), inc=True):
        self._emit_waits(e, self._deps(reads, writes))
        if inc:
            self.count[e] += 1
            tok = (("eng", e), self.count[e])
            h = self.sem[e]
            self.prog[e].append(lambda eng, fn=fn, h=h: fn(eng).then_inc(h, 1))
            self.meta[e].append(("inc", ("eng", e), 1))
        else:
            tok = (("eng", e), self.count[e] + 1)
            self.prog[e].append(lambda eng, fn=fn: fn(eng))
        self.n_inst[e] += 1
        self._commit(tok, reads, writes)
        return tok

    def dma(self, q, sem_id, sem_handle, fn, reads=(), writes=()):
        self.semobj[sem_id] = sem_handle
        self._emit_waits(q, self._deps(reads, writes))
        v = self.dma_count.get(sem_id, 0) + 16
        self.dma_count[sem_id] = v
        tok = (sem_id, v)
        self.prog[q].append(lambda eng, fn=fn, h=sem_handle: fn(eng).then_inc(h, 16))
        self.meta[q].append(("inc", sem_id, 16))
        self.n_inst[q] += 1
        self._commit(tok, reads, writes)
        return tok

    def wait_tokens(self, e, toks):
        need = {}
        for sid, v in toks:
            if need.get(sid, 0) < v:
                need[sid] = v
        self._emit_waits(e, need)

    def replay(self, block):
        m = dict(pe=block.tensor, act=block.scalar, dve=block.vector, pool=block.gpsimd, sp=block.sync)
        for e in self.ENG:
            lst = self.prog[e]

            def body(eng, lst=lst):
                for th in lst:
                    th(eng)
            m[e](body)


def _param_names():
    names = []
    for i in range(DEPTH):
        names += [f"gmix{i}", f"gffn{i}"] + [f"ada{i}_{k}" for k in range(6)]
    for j in range(2):
        names += [f"sgu_bu{j}", f"sgu_bv{j}", f"sgu_vg{j}", f"sgu_vb{j}", f"sgu_bo{j}"]
    names += ["cv_ba", "cv_bb", "cv_bdw", "cv_lg", "cv_lb", "cv_b2", "pl_b", "pl_s", "final_g", "c"]
    return names


PNAMES = _param_names()
PIDX = {n: i for i, n in enumerate(PNAMES)}
NV = len(PNAMES)


def build_program(NT, n_layers=DEPTH, NH=2, HW=512, PW=256, NS=4):
    T = NH * HW
    TB = HALO + T
    TOK = HALO + NT * T
    NCH = PW // 128
    nc = bass.Bass("TRN2", target_bir_lowering=False)

    def din(name, shape):
        return nc.dram_tensor(name, shape, F32, kind="ExternalInput").ap()

    xT = din("xT", [D, TOK])
    yT = nc.dram_tensor("yT", [D, NT * T], F32, kind="ExternalOutput").ap()
    prm_d = din("prm", [128, NV * KC])
    wdw_d = din("wdw", [128, KC * CONVW])
    wsT_d = din("wsT", [2, 128, D])
    bs_d = din("bs", [2, 1, D])
    cst_d = din("cst", [3, 128, 128])
    misc_d = din("misc", [128, 1 + 64])
    w_ada = din("w_ada", [DEPTH, D, 6 * D])
    sgu_w_in = din("sgu_w_in", [2, D, 2 * D])
    sgu_w_out = din("sgu_w_out", [2, D, D])
    conv_w_pw1 = din("conv_w_pw1", [1, D, 2 * D])
    conv_w_pw2 = din("conv_w_pw2", [1, D, D])
    pool_w = din("pool_w", [1, 4, 512, 512])
    mlp_w1 = din("mlp_w1", [DEPTH, D, DFF])
    mlp_w2 = din("mlp_w2", [DEPTH, DFF, D])

    from contextlib import ExitStack
    es = ExitStack()

    def sb(name, shape, dt):
        return es.enter_context(nc.sbuf_tensor(name, shape, dt))

    def sem(name):
        return es.enter_context(nc.semaphore(name))

    X = sb("X", [128, KC, TB], F32)
    A = sb("A", [128, KC, TB], BF16)
    Bb = sb("Bb", [128, KC, TB], BF16)
    W = sb("W", [128, NS, KC * PW], BF16)
    PRM = sb("PRM", [128, NV, KC], F32)
    MOD = sb("MOD", [128, DEPTH, 9, KC], F32)
    WDW = sb("WDW", [128, KC, CONVW], F32)
    MISC = sb("MISC", [128, 65], F32)
    identb = sb("identb", [128, 128], BF16)
    cmaskb = sb("cmaskb", [128, 128], BF16)
    onesb = sb("onesb", [128, 128], BF16)
    cact = sb("cact", [128, KC], BF16)
    epsT = sb("epsT", [128, 1], F32)
    ATAIL = sb("ATAIL", [128, KC, 32], BF16)
    HTAIL = sb("HTAIL", [128, KC, 16], BF16)
    NTMP = 4
    NTMPB = 4
    TF = sb("TF", [128, NTMP, HW + 16], F32)
    TBF = sb("TBF", [128, NTMPB, HW], BF16)
    ST = sb("ST", [128, 4, HW], F32)
    NBANK = 6
    ps = es.enter_context(nc.psum_tensor("ps", [128, NBANK, 512], F32))
    pst = es.enter_context(nc.psum_tensor("pst", [128, 2, 1024], BF16))

    S = Sched(dict(pe=sem("s_pe"), act=sem("s_act"), dve=sem("s_dve"), pool=sem("s_pool"), sp=sem("s_sp")))
    wsem = [sem(f"s_w{i}") for i in range(NS)]
    xsems = [sem(f"s_x{h}") for h in range(NH)]
    osem = [sem(f"s_o{i}") for i in range(NTMP)]
    csem = sem("s_c")

    st = dict(bank=0, slot=0, tf=0, tb=0, pstb=0, ot=0, layer=0)

    reserved = set()

    def next_bank():
        while True:
            b = st["bank"]
            st["bank"] = (b + 1) % NBANK
            if b not in reserved:
                return b

    rs = dict(banks={}, cnt={}, pend=[])

    def rs_begin(sbs_own):
        assert not rs["banks"]
        for (sid, off, w) in sbs_own:
            b = next_bank()
            reserved.add(b)
            rs["banks"][sid] = b
            rs["cnt"][sid] = 0

    def rs_emit_one():
        bank, ti, w, first, last = rs["pend"].pop(0)
        S.op("pe", lambda e: e.matmul(ps[:, bank, 0:w], lhsT=onesb[:, :], rhs=TBF[:, ti, 0:w], start=first, stop=last),
             reads=[("tb", ti), "onesb"], writes=[("ps", bank)], inc=True)

    def rs_feed(m, sid, off, w):
        if sid not in rs["banks"]:
            return
        bank = rs["banks"][sid]
        c = rs["cnt"][sid]
        rs["cnt"][sid] = c + 1
        ti = next_tb()
        S.op("act", lambda e: e.activation(out=TBF[:, ti, 0:w], in_=X[:, m, off:off + w], func=AF.Square),
             reads=[("X", m, sid)], writes=[("tb", ti)])
        rs["pend"].append((bank, ti, w, c == 0, c == KC - 1))
        while len(rs["pend"]) > 2:
            rs_emit_one()

    def rs_take(sid):
        if sid not in rs["banks"]:
            return None
        while rs["pend"]:
            rs_emit_one()
        assert rs["cnt"][sid] == KC, (sid, rs["cnt"][sid])
        b = rs["banks"].pop(sid)
        rs["cnt"].pop(sid)
        return b

    ls = dict(banks={}, cnt={}, pend=[])

    def ls_begin(sbs_own):
        assert not ls["banks"]
        for (sid, off, w) in sbs_own:
            b1 = next_bank()
            reserved.add(b1)
            b2 = next_bank()
            reserved.add(b2)
            ls["banks"][sid] = (b1, b2)
            ls["cnt"][sid] = 0

    def ls_emit_one():
        b1, b2, ti, src_ap, rd, w, first, last = ls["pend"].pop(0)
        S.op("pe", lambda e: e.matmul(ps[:, b1, 0:w], lhsT=onesb[:, :], rhs=src_ap, start=first, stop=last),
             reads=rd + ["onesb"], writes=[("ps", b1)], inc=True)
        S.op("pe", lambda e: e.matmul(ps[:, b2, 0:w], lhsT=onesb[:, :], rhs=TBF[:, ti, 0:w], start=first, stop=last),
             reads=[("tb", ti), "onesb"], writes=[("ps", b2)], inc=True)

    def ls_flush():
        while ls["pend"]:
            ls_emit_one()

    def ls_feed(buf, bname, m, sid, off, w):
        if sid not in ls["banks"]:
            return
        b1, b2 = ls["banks"][sid]
        c = ls["cnt"][sid]
        ls["cnt"][sid] = c + 1
        ti = next_tb()
        S.op("act", lambda e: e.activation(out=TBF[:, ti, 0:w], in_=buf[:, m, off:off + w], func=AF.Square),
             reads=[(bname, m, sid)], writes=[("tb", ti)])
        ls["pend"].append((b1, b2, ti, buf[:, m, off:off + w], [(bname, m, sid)], w, c == 0, c == KC - 1))
        while len(ls["pend"]) > 2:
            ls_emit_one()

    def ls_take(sid):
        if sid not in ls["banks"]:
            return None
        ls_flush()
        assert ls["cnt"][sid] == KC
        ls["cnt"].pop(sid)
        return ls["banks"].pop(sid)

    pinned = set()

    def next_slot():
        while True:
            s = st["slot"]
            st["slot"] = (s + 1) % NS
            if s not in pinned:
                return s

    def next_tf():
        i = st["tf"]
        st["tf"] = (i + 1) % NTMP
        return i

    def next_tb():
        i = st["tb"]
        st["tb"] = (i + 1) % NTMPB
        return i

    def P(name, m=None):
        i = PIDX[name]
        if m is None:
            return PRM[:, i, :]
        return PRM[:, i, m:m + 1]

    Wv = lambda slot: W[:, slot, :].rearrange("p (k n) -> p k n", k=KC)

    S.dma("sp", ("c",), csem, lambda e: e.dma_start(out=PRM[:, :, :], in_=prm_d.rearrange("p (v c) -> p v c", v=NV)),
          writes=["PRM"])
    S.dma("sp", ("c",), csem, lambda e: e.dma_start(out=WDW[:, :, :], in_=wdw_d.rearrange("p (m k) -> p m k", m=KC)),
          writes=["WDW"])
    S.dma("sp", ("c",), csem, lambda e: e.dma_start(out=MISC[:, :], in_=misc_d[:, :]), writes=["MISC"])
    S.dma("pool", ("c",), csem, lambda e: e.dma_start(out=cmaskb[:, :], in_=cst_d[0]), writes=["cmaskb"])
    S.dma("pool", ("c",), csem, lambda e: e.dma_start(out=identb[:, :], in_=cst_d[1]), writes=["identb"])
    ctok = (("c",), S.dma_count[("c",)])
    for k in ("PRM", "WDW", "MISC", "cmaskb", "identb"):
        S.lastw[k] = ctok
    S.op("dve", lambda e: e.memset(onesb[:, :], 1.0), writes=["onesb"])
    S.op("dve", lambda e: e.memset(epsT[:, :], EPS), writes=["epsT"])
    S.op("dve", lambda e: e.memset(ATAIL[:, :, :], 0.0), writes=["ATAIL"])
    S.op("dve", lambda e: e.memset(HTAIL[:, :, :], 0.0), writes=["HTAIL"])
    HM = MISC[:, 0:1]
    INVC = MISC[:, 1:65].rearrange("p (g t) -> p g t", g=4)

    def load_piece(src3):
        slot = next_slot()
        S.dma("pool", ("w", slot), wsem[slot],
              lambda e, slot=slot, src3=src3: e.dma_start(out=Wv(slot), in_=src3),
              writes=[("w", slot)])
        return slot

    def wpiece(wd2, r0, c0):
        return wd2[r0:r0 + D, :].rearrange("(k p) n -> p k n", p=128)[:, :, c0:c0 + PW]

    def mm_group(lhs, rhs, nk, w, rd):
        bank = next_bank()
        for k in range(nk):
            S.op("pe", lambda e, k=k, bank=bank: e.matmul(ps[:, bank, 0:w], lhsT=lhs(k), rhs=rhs(k),
                                                          start=(k == 0), stop=(k == nk - 1)),
                 reads=rd(k), writes=[("ps", bank)], inc=(k == nk - 1))
        return bank

    def stats_sum(src_fn, rd_fn, w, square):
        bank = next_bank()
        for m in range(KC):
            if square:
                ti = next_tb()
                if m % 2 == 0 or w >= HW:
                    S.op("act", lambda e, m=m, ti=ti: e.activation(out=TBF[:, ti, 0:w], in_=src_fn(m), func=AF.Square),
                         reads=rd_fn(m), writes=[("tb", ti)])
                else:
                    S.op("dve", lambda e, m=m, ti=ti: e.tensor_tensor(out=TBF[:, ti, 0:w], in0=src_fn(m), in1=src_fn(m),
                                                                      op=ALU.mult),
                         reads=rd_fn(m), writes=[("tb", ti)])
                S.op("pe", lambda e, m=m, ti=ti, bank=bank: e.matmul(ps[:, bank, 0:w], lhsT=onesb[:, :], rhs=TBF[:, ti, 0:w],
                                                                     start=(m == 0), stop=(m == KC - 1)),
                     reads=[("tb", ti), "onesb"], writes=[("ps", bank)], inc=True)
            else:
                S.op("pe", lambda e, m=m, bank=bank: e.matmul(ps[:, bank, 0:w], lhsT=onesb[:, :], rhs=src_fn(m),
                                                              start=(m == 0), stop=(m == KC - 1)),
                     reads=rd_fn(m) + ["onesb"], writes=[("ps", bank)], inc=(m == KC - 1))
        return bank

    def rstd_from(bank_or_ap_fn, w, dst_i, reads):
        S.op("act", lambda e: e.activation(out=ST[:, dst_i, 0:w], in_=bank_or_ap_fn(), func=AF.Ln,
                                           bias=epsT[:, 0:1], scale=1.0 / D),
             reads=reads + ["epsT"], writes=[("st", dst_i)])
        S.op("act", lambda e: e.activation(out=ST[:, dst_i, 0:w], in_=ST[:, dst_i, 0:w], func=AF.Exp, scale=-0.5),
             reads=[("st", dst_i)], writes=[("st", dst_i)])

    def rmsnorm_mod(sbs, gs_fn, sh_fn, dst, dname):
        S.phase("norm")
        while rs["pend"]:
            rs_emit_one()
        for (sid, off, w) in sbs:
            bank = rs_take(sid)
            if bank is None:
                bank = stats_sum(lambda m, off=off, w=w: X[:, m, off:off + w], lambda m, sid=sid: [("X", m, sid)], w, True)
            rstd_from(lambda bank=bank, w=w: ps[:, bank, 0:w], w, 0, [("ps", bank)])
            reserved.discard(bank)
            for m in range(KC):
                ti = next_tf()
                S.op("dve", lambda e, m=m, ti=ti, off=off, w=w: e.scalar_tensor_tensor(
                    out=TF[:, ti, 0:w], in0=X[:, m, off:off + w], scalar=gs_fn(m), in1=ST[:, 0, 0:w],
                    op0=ALU.mult, op1=ALU.mult),
                    reads=[("X", m, sid), ("st", 0), MK()], writes=[("tf", ti)])
                S.op("act", lambda e, m=m, ti=ti, off=off, w=w: e.activation(
                    out=dst[:, m, off:off + w], in_=TF[:, ti, 0:w], func=AF.Identity, bias=sh_fn(m), scale=1.0),
                    reads=[("tf", ti), MK()], writes=[(dname, m, sid)])

    def ln_stats(buf, bname, sid, lo, w, base=0):
        pre = ls_take(sid)
        if pre is None:
            b1 = stats_sum(lambda m: buf[:, m, lo:lo + w], lambda m: [(bname, m, sid)], w, False)
            b2 = stats_sum(lambda m: buf[:, m, lo:lo + w], lambda m: [(bname, m, sid)], w, True)
        else:
            b1, b2 = pre
        t4, t5, t6 = next_tf(), next_tf(), next_tf()
        S.op("act", lambda e: e.activation(out=TF[:, t6, 0:w], in_=ps[:, b1, 0:w], func=AF.Copy, scale=1.0 / D),
             reads=[("ps", b1)], writes=[("tf", t6)])
        S.op("dve", lambda e: e.tensor_tensor(out=TF[:, t4, 0:w], in0=TF[:, t6, 0:w], in1=TF[:, t6, 0:w], op=ALU.mult),
             reads=[("tf", t6)], writes=[("tf", t4)])
        S.op("dve", lambda e: e.scalar_tensor_tensor(out=TF[:, t5, 0:w], in0=ps[:, b2, 0:w], scalar=1.0 / D,
                                                     in1=TF[:, t4, 0:w], op0=ALU.mult, op1=ALU.subtract),
             reads=[("ps", b2), ("tf", t4)], writes=[("tf", t5)])
        S.op("dve", lambda e: e.tensor_scalar(out=TF[:, t5, 0:w], in0=TF[:, t5, 0:w], scalar1=0.0, scalar2=None,
                                              op0=ALU.max),
             reads=[("tf", t5)], writes=[("tf", t5)])
        S.op("act", lambda e: e.activation(out=ST[:, base, 0:w], in_=TF[:, t5, 0:w], func=AF.Ln,
                                           bias=epsT[:, 0:1], scale=1.0),
             reads=[("tf", t5), "epsT"], writes=[("st", base)])
        S.op("act", lambda e: e.activation(out=ST[:, base, 0:w], in_=ST[:, base, 0:w], func=AF.Exp, scale=-0.5),
             reads=[("st", base)], writes=[("st", base)])
        S.op("dve", lambda e: e.scalar_tensor_tensor(out=ST[:, base + 1, 0:w], in0=TF[:, t6, 0:w], scalar=-1.0,
                                                     in1=ST[:, base, 0:w], op0=ALU.mult, op1=ALU.mult),
             reads=[("tf", t6), ("st", base)], writes=[("st", base + 1)])
        reserved.discard(b1)
        reserved.discard(b2)

    def gelu_evac(bank, w, bias_ap, out_ap, wr):
        if GELU_NATIVE:
            S.op("act", lambda e: e.activation(out=out_ap, in_=ps[:, bank, 0:w], func=AF.Gelu_apprx_tanh,
                                               bias=bias_ap, scale=1.0),
                 reads=[("ps", bank), "PRM"], writes=wr)
            return
        t0, t1 = next_tf(), next_tf()
        S.op("act", lambda e: e.activation(out=TF[:, t0, 0:w], in_=ps[:, bank, 0:w], func=AF.Identity,
                                           bias=bias_ap, scale=1.0),
             reads=[("ps", bank), "PRM"], writes=[("tf", t0)])
        S.op("dve", lambda e: e.tensor_tensor(out=TF[:, t1, 0:w], in0=TF[:, t0, 0:w], in1=TF[:, t0, 0:w], op=ALU.mult),
             reads=[("tf", t0)], writes=[("tf", t1)])
        S.op("dve", lambda e: e.tensor_scalar(out=TF[:, t1, 0:w], in0=TF[:, t1, 0:w], scalar1=0.044715, scalar2=1.0,
                                              op0=ALU.mult, op1=ALU.add),
             reads=[("tf", t1)], writes=[("tf", t1)])
        S.op("dve", lambda e: e.tensor_tensor(out=TF[:, t1, 0:w], in0=TF[:, t1, 0:w], in1=TF[:, t0, 0:w], op=ALU.mult),
             reads=[("tf", t1), ("tf", t0)], writes=[("tf", t1)])
        S.op("act", lambda e: e.activation(out=TF[:, t1, 0:w], in_=TF[:, t1, 0:w], func=AF.Sigmoid,
                                           scale=1.5957691216057308),
             reads=[("tf", t1)], writes=[("tf", t1)])
        S.op("dve", lambda e: e.tensor_tensor(out=out_ap, in0=TF[:, t1, 0:w], in1=TF[:, t0, 0:w], op=ALU.mult),
             reads=[("tf", t1), ("tf", t0)], writes=wr)

    def resid_evac(bank, w, bias_ap, gate_ap, m, sid, off):
        ti = next_tf()
        S.op("act", lambda e: e.activation(out=TF[:, ti, 0:w], in_=ps[:, bank, 0:w], func=AF.Identity,
                                           bias=bias_ap, scale=1.0),
             reads=[("ps", bank), "PRM"], writes=[("tf", ti)])
        S.op("dve", lambda e: e.scalar_tensor_tensor(out=X[:, m, off:off + w], in0=TF[:, ti, 0:w], scalar=gate_ap,
                                                     in1=X[:, m, off:off + w], op0=ALU.mult, op1=ALU.add),
             reads=[("tf", ti), ("X", m, sid), MK()], writes=[("X", m, sid)])
        rs_feed(m, sid, off, w)

    def piece_loop(npieces, load_fn, body_fn, sbs, lead=0, nch=None, bg=False):
        nch = NCH if nch is None else nch
        lead = min(lead, npieces) if len(sbs) > 1 else 0
        slots = [load_fn(pc) for pc in range(lead)]
        for (sid, off, w) in sbs:
            for pc in range(lead):
                for cc in range(nch):
                    body_fn(pc, slots[pc], cc, sid, off, w)
        for pc in range(lead, npieces):
            slot = load_fn(pc)
            for cc in range(nch):
                for (sid, off, w) in sbs:
                    body_fn(pc, slot, cc, sid, off, w)
            if bg:
                mod_bg(1)

    def dense_out(wd2, src, sname, sbs, bias_name, gate_fn, lead=0, bg=False):
        def body(pc, slot, cc, sid, off, w):
            m = pc * NCH + cc
            bank = mm_group(lambda k: Wv(slot)[:, k, cc * 128:(cc + 1) * 128],
                            lambda k: src[:, k, off:off + w], KC, w,
                            lambda k: [("w", slot), (sname, k, sid)])
            resid_evac(bank, w, P(bias_name, m), gate_fn(m), m, sid, off)
        rs_begin([sb_ for sb_ in sbs if sb_[0] > 0])
        piece_loop(D // PW, lambda pc: load_piece(wpiece(wd2, 0, pc * PW)), body, sbs, lead=lead, bg=bg)

    S.op("act", lambda e: e.activation(out=cact[:, :], in_=P("c"), func=AF.Silu), reads=["PRM"], writes=["cact"])
    from collections import deque
    mod_jobs = deque()
    mod_done = set()

    def MK():
        return ("MOD", st["layer"])

    def mod_piece(i, pc):
        slot = load_piece(wpiece(w_ada[i], 0, pc * PW))
        bank = next_bank()
        for cc in range(NCH):
            for k in range(KC):
                S.op("pe", lambda e, slot=slot, cc=cc, k=k, bank=bank: e.matmul(
                    ps[:, bank, cc:cc + 1], lhsT=Wv(slot)[:, k, cc * 128:(cc + 1) * 128], rhs=cact[:, k:k + 1],
                    start=(k == 0), stop=(k == KC - 1)),
                    reads=[("w", slot), "cact"], writes=[("ps", bank)], inc=(k == KC - 1))
        s_, c_ = divmod(pc * NCH, KC)
        a0 = PIDX[f"ada{i}_0"]
        S.op("dve", lambda e, i=i, bank=bank, a0=a0, s_=s_, c_=c_: e.tensor_tensor(
            out=MOD[:, i, s_, c_:c_ + NCH], in0=ps[:, bank, 0:NCH], in1=PRM[:, a0 + s_, c_:c_ + NCH], op=ALU.add),
            reads=[("ps", bank), "PRM"], writes=[("MODraw", i, pc)])

    NPC_HALF = 3 * D // PW

    def mod_finish(i, part):
        if part == "m":
            raw = [("MODraw", i, pc) for pc in range(NPC_HALF)]
            S.op("dve", lambda e, i=i: e.scalar_tensor_tensor(out=MOD[:, i, 6, :], in0=MOD[:, i, 1, :], scalar=1.0,
                                                              in1=P(f"gmix{i}"), op0=ALU.add, op1=ALU.mult),
                 reads=raw + ["PRM"], writes=[("MOD", i)])
            if i == 2:
                S.op("dve", lambda e, i=i: e.tensor_tensor(out=MOD[:, i, 8, :], in0=MOD[:, i, 2, :], in1=P("pl_s"),
                                                           op=ALU.mult),
                     reads=raw + ["PRM"], writes=[("MOD", i)])
        else:
            raw = [("MODraw", i, pc) for pc in range(NPC_HALF, 2 * NPC_HALF)]
            S.op("dve", lambda e, i=i: e.scalar_tensor_tensor(out=MOD[:, i, 7, :], in0=MOD[:, i, 4, :], scalar=1.0,
                                                              in1=P(f"gffn{i}"), op0=ALU.add, op1=ALU.mult),
                 reads=raw + ["PRM"], writes=[("MOD", i)])
        mod_done.add((i, part))

    for i in range(n_layers):
        for pc in range(6 * D // PW):
            mod_jobs.append((i, pc))

    def mod_bg(n=1):
        for _ in range(n):
            if not mod_jobs:
                return
            i, pc = mod_jobs.popleft()
            mod_piece(i, pc)

    def mod_drain(i, part):
        lim = NPC_HALF if part == "m" else 2 * NPC_HALF
        while mod_jobs and (mod_jobs[0][0] < i or (mod_jobs[0][0] == i and mod_jobs[0][1] < lim)):
            mod_bg(1)
        if (i, "m") not in mod_done:
            mod_finish(i, "m")
        if part == "f" and (i, "f") not in mod_done:
            mod_finish(i, "f")

    mod_drain(0, "m")

    def Mod(i, s, m):
        return MOD[:, i, s, m:m + 1]

    def ffn(i, sbs, bg=False):
        st["layer"] = i
        S.phase(f"ffn{i}")
        rmsnorm_mod(sbs, lambda m: Mod(i, 7, m), lambda m: Mod(i, 3, m), A, "A")
        for q in range(4):
            S.phase(f"ffn{i} w1 q{q}")

            def body1(pc, slot, cc, sid, off, w):
                hc = pc * NCH + cc
                bank = mm_group(lambda k: Wv(slot)[:, k, cc * 128:(cc + 1) * 128],
                                lambda k: A[:, k, off:off + w], KC, w,
                                lambda k: [("w", slot), ("A", k, sid)])
                ti = next_tf()
                S.op("act", lambda e: e.activation(out=TF[:, ti, 0:w], in_=ps[:, bank, 0:w], func=AF.Relu),
                     reads=[("ps", bank)], writes=[("tf", ti)])
                S.op("dve", lambda e: e.tensor_tensor(out=Bb[:, hc, off:off + w], in0=TF[:, ti, 0:w],
                                                      in1=TF[:, ti, 0:w], op=ALU.mult),
                     reads=[("tf", ti)], writes=[("Bb", hc, sid)])
            piece_loop(D // PW, lambda pc, q=q: load_piece(wpiece(mlp_w1[i], 0, q * D + pc * PW)), body1, sbs,
                       lead=(3 if q == 0 else 0), bg=bg)
            S.phase(f"ffn{i} w2 q{q}")

            def body2(pc, slot, cc, sid, off, w):
                m = pc * NCH + cc
                bank = mm_group(lambda k: Wv(slot)[:, k, cc * 128:(cc + 1) * 128],
                                lambda k: Bb[:, k, off:off + w], KC, w,
                                lambda k: [("w", slot), ("Bb", k, sid)])
                S.op("dve", lambda e: e.scalar_tensor_tensor(
                    out=X[:, m, off:off + w], in0=ps[:, bank, 0:w], scalar=Mod(i, 5, m),
                    in1=X[:, m, off:off + w], op0=ALU.mult, op1=ALU.add),
                    reads=[("ps", bank), ("X", m, sid), MK()], writes=[("X", m, sid)])
                if q == 3:
                    rs_feed(m, sid, off, w)
            if q == 3:
                rs_begin([sb_ for sb_ in sbs if sb_[0] > 0])
            piece_loop(D // PW, lambda pc, q=q: load_piece(wpiece(mlp_w2[i], q * D, pc * PW)), body2, sbs, bg=bg)

    def sgu(i, j, sbs, bg=False):
        st["layer"] = i
        rmsnorm_mod(sbs, lambda m: Mod(i, 6, m), lambda m: Mod(i, 0, m), A, "A")
        S.phase("sgu v")

        def bodyv(pc, slot, cc, sid, off, w):
            m = pc * NCH + cc
            bank = mm_group(lambda k: Wv(slot)[:, k, cc * 128:(cc + 1) * 128],
                            lambda k: A[:, k, off:off + w], KC, w,
                            lambda k: [("w", slot), ("A", k, sid)])
            gelu_evac(bank, w, P(f"sgu_bv{j}", m), Bb[:, m, off:off + w], [("Bb", m, sid)])
            ls_feed(Bb, "Bb", m, sid, off, w)
        ls_begin([sb_ for sb_ in sbs if sb_[0] > 0])
        piece_loop(D // PW, lambda pc: load_piece(wpiece(sgu_w_in[j], 0, D + pc * PW)), bodyv, sbs, lead=3, bg=bg)
        ls_flush()
        S.phase("sgu spatial")
        slot = next_slot()
        pinned.add(slot)
        S.dma("pool", ("w", slot), wsem[slot], lambda e, slot=slot: e.dma_start(out=W[:, slot, 0:D].rearrange("p (h q) -> p h q", h=KC),
                                                                             in_=wsT_d[j].rearrange("p (h q) -> p h q", h=KC)),
              writes=[("w", slot)])
        S.dma("pool", ("w", slot), wsem[slot], lambda e, slot=slot: e.dma_start(out=W[0:1, slot, D:2 * D].rearrange("p (h q) -> p h q", h=KC),
                                                                             in_=bs_d[j].rearrange("p (h q) -> p h q", h=KC)),
              writes=[("w", slot)])
        S.op("dve", lambda e, slot=slot: e.tensor_tensor(
            out=W[:, slot, 0:D].rearrange("p (h q) -> p h q", h=KC),
            in0=W[:, slot, 0:D].rearrange("p (h q) -> p h q", h=KC),
            in1=cmaskb[:, :].unsqueeze(1).broadcast_to([128, KC, 128]), op=ALU.mult),
            reads=[("w", slot), "cmaskb"], writes=[("w", slot)])
        def emit_A(m, sid, off, w, base):
            t1, t2 = next_tf(), next_tf()
            S.op("dve", lambda e: e.scalar_tensor_tensor(
                out=TF[:, t1, 0:w], in0=Bb[:, m, off:off + w], scalar=P(f"sgu_vg{j}", m), in1=ST[:, base, 0:w],
                op0=ALU.mult, op1=ALU.mult),
                reads=[("Bb", m, sid), ("st", base), "PRM"], writes=[("tf", t1)])
            S.op("dve", lambda e: e.scalar_tensor_tensor(
                out=TF[:, t2, 0:w], in0=ST[:, base + 1, 0:w], scalar=P(f"sgu_vg{j}", m), in1=TF[:, t1, 0:w],
                op0=ALU.mult, op1=ALU.add),
                reads=[("st", base + 1), ("tf", t1), "PRM"], writes=[("tf", t2)])
            S.op("act", lambda e: e.activation(
                out=Bb[:, m, off:off + w], in_=TF[:, t2, 0:w], func=AF.Identity, bias=P(f"sgu_vb{j}", m), scale=1.0),
                reads=[("tf", t2), "PRM"], writes=[("Bb", m, sid)])

        def emit_T(m, sid, off, w):
            nn = w // 128
            tp = st["pstb"]
            st["pstb"] = 1 - tp
            for n in range(nn):
                S.op("pe", lambda e, n=n: e.transpose(
                    out=pst[:, tp, n * 128:(n + 1) * 128], in_=Bb[:, m, off + n * 128:off + (n + 1) * 128],
                    identity=identb[:, :]),
                    reads=[("Bb", m, sid), "identb"], writes=[("pst", tp)], inc=(n == nn - 1))
            return tp

        def emit_C(w, tp):
            ti = next_tb()
            S.op("act", lambda e: e.activation(out=TBF[:, ti, 0:w], in_=pst[:, tp, 0:w], func=AF.Copy),
                 reads=[("pst", tp)], writes=[("tb", ti)])
            return ti

        def emit_M(m, sid, off, w, ti):
            nn = w // 128
            bank = next_bank()
            for n in range(nn):
                S.op("pe", lambda e, n=n, sl=slot: e.matmul(
                    ps[:, bank, n * 128:(n + 1) * 128], lhsT=TBF[:, ti, n * 128:(n + 1) * 128],
                    rhs=W[:, sl, m * 128:(m + 1) * 128], start=True, stop=False),
                    reads=[("tb", ti), ("w", slot)], writes=[("ps", bank)], inc=True)
                S.op("pe", lambda e, n=n, sl=slot: e.matmul(
                    ps[:, bank, n * 128:(n + 1) * 128], lhsT=onesb[0:1, :],
                    rhs=W[0:1, sl, D + m * 128:D + (m + 1) * 128], start=False, stop=True),
                    reads=[("w", slot), "onesb"], writes=[("ps", bank)], inc=(n == nn - 1))
            S.op("dve", lambda e: e.tensor_copy(out=Bb[:, m, off:off + w], in_=ps[:, bank, 0:w]),
                 reads=[("ps", bank)], writes=[("Bb", m, sid)])

        def emit_U(m, uslot, cc):
            for (sid, off, w) in sbs:
                bank = mm_group(lambda k: Wv(uslot)[:, k, cc * 128:(cc + 1) * 128],
                                lambda k, off=off, w=w: A[:, k, off:off + w], KC, w,
                                lambda k, sid=sid: [("w", uslot), ("A", k, sid)])
                tu = next_tf()
                gelu_evac(bank, w, P(f"sgu_bu{j}", m), TF[:, tu, 0:w], [("tf", tu)])
                S.op("dve", lambda e, tu=tu, off=off, w=w: e.tensor_tensor(
                    out=Bb[:, m, off:off + w], in0=TF[:, tu, 0:w], in1=Bb[:, m, off:off + w], op=ALU.mult),
                    reads=[("tf", tu), ("Bb", m, sid)], writes=[("Bb", m, sid)])

        own_sbs = [sb_ for sb_ in sbs if sb_[0] > 0]
        halo_sbs = [sb_ for sb_ in sbs if sb_[0] == 0]
        S.phase("sgu LN")
        for (sid, off, w) in halo_sbs:
            ln_stats(Bb, "Bb", sid, off, w, base=0)
            for m in range(KC):
                emit_A(m, sid, off, w, 0)
                tp = emit_T(m, sid, off, w)
                ti = emit_C(w, tp)
                emit_M(m, sid, off, w, ti)
        bases = {}
        for idx, (sid, off, w) in enumerate(own_sbs):
            ln_stats(Bb, "Bb", sid, off, w, base=2 * idx)
            bases[sid] = 2 * idx
        S.phase("sgu pipe")
        uslots = {}

        def uslot_of(pc):
            if pc not in uslots:
                uslots[pc] = load_piece(wpiece(sgu_w_in[j], 0, pc * PW))
            return uslots[pc]
        for pc in range(min(2, D // PW)):
            uslot_of(pc)
        prevC = []
        for it in range(KC + 2):
            m = it
            if m < KC:
                for (sid, off, w) in own_sbs:
                    emit_A(m, sid, off, w, bases[sid])
            if it >= 2:
                mu = it - 2
                emit_U(mu, uslot_of(mu // NCH), mu % NCH)
            curC = []
            if m < KC:
                tps = [(sb_, emit_T(m, *sb_)) for sb_ in own_sbs]
                curC = [(sb_, emit_C(sb_[2], tp)) for (sb_, tp) in tps]
            for (sb_, ti) in prevC:
                emit_M(m - 1, sb_[0], sb_[1], sb_[2], ti)
            prevC = curC
        pinned.discard(slot)
        S.phase("sgu out")
        dense_out(sgu_w_out[j], Bb, "Bb", sbs, f"sgu_bo{j}", lambda m: Mod(i, 2, m), bg=bg)

    def conv(i, sbs, first_tile):
        st["layer"] = i
        rmsnorm_mod(sbs, lambda m: Mod(i, 6, m), lambda m: Mod(i, 0, m), A, "A")
        w1v = conv_w_pw1[0].rearrange("(k p) (two n) -> p k two n", p=128, two=2)
        S.phase("conv pw1")
        for m in range(KC):
            slot = next_slot()
            for hf in range(2):
                S.dma("pool", ("w", slot), wsem[slot],
                      lambda e, slot=slot, m=m, hf=hf: e.dma_start(
                          out=W[:, slot, 0:KC * 256].rearrange("p (k two n) -> p k two n", k=KC, two=2)[:, :, hf, :],
                          in_=w1v[:, :, hf, m * 128:(m + 1) * 128]),
                      writes=[("w", slot)])
            wv4 = lambda slot: W[:, slot, 0:KC * 256].rearrange("p (k two n) -> p k two n", k=KC, two=2)
            for (sid, off, w) in sbs:
                bankA = mm_group(lambda k, slot=slot: wv4(slot)[:, k, 0, :],
                                 lambda k, off=off, w=w: A[:, k, off:off + w], KC, w,
                                 lambda k, slot=slot, sid=sid: [("w", slot), ("A", k, sid)])
                bankB = mm_group(lambda k, slot=slot: wv4(slot)[:, k, 1, :],
                                 lambda k, off=off, w=w: A[:, k, off:off + w], KC, w,
                                 lambda k, slot=slot, sid=sid: [("w", slot), ("A", k, sid)])
                ti = next_tf()
                S.op("act", lambda e, m=m, ti=ti, bankB=bankB, w=w: e.activation(
                    out=TF[:, ti, 0:w], in_=ps[:, bankB, 0:w], func=AF.Sigmoid, bias=P("cv_bb", m), scale=1.0),
                    reads=[("ps", bankB), "PRM"], writes=[("tf", ti)])
                S.op("dve", lambda e, m=m, ti=ti, bankA=bankA, off=off, w=w: e.scalar_tensor_tensor(
                    out=Bb[:, m, off:off + w], in0=ps[:, bankA, 0:w], scalar=P("cv_ba", m), in1=TF[:, ti, 0:w],
                    op0=ALU.add, op1=ALU.mult),
                    reads=[("ps", bankA), ("tf", ti), "PRM"], writes=[("Bb", m, sid)])
        allm0 = [("Bb", m, 0) for m in range(KC)]
        lastsid = sbs[-1][0]
        if first_tile:
            S.op("dve", lambda e: e.tensor_scalar(out=Bb[:, :, 0:HALO], in0=Bb[:, :, 0:HALO], scalar1=HM, scalar2=None,
                                                  op0=ALU.mult),
                 reads=allm0 + ["MISC"], writes=allm0)
        else:
            S.op("dve", lambda e: e.tensor_copy(out=Bb[:, :, HALO - 32:HALO], in_=ATAIL[:, :, :]),
                 reads=["ATAIL"], writes=allm0)
        S.op("dve", lambda e: e.tensor_copy(out=ATAIL[:, :, :], in_=Bb[:, :, TB - 32:TB]),
             reads=[("Bb", m, lastsid) for m in range(KC)], writes=["ATAIL"])
        ls_begin([sb_ for sb_ in sbs if sb_[0] > 0])
        S.phase("conv dw")
        for m in range(KC):
            slot = next_slot()
            dv = lambda slot: W[:, slot, 0:CONVW * 128].rearrange("p (k n) -> p k n", k=CONVW)
            S.op("dve", lambda e, slot=slot, m=m: e.tensor_tensor(
                out=dv(slot), in0=identb[:, :].unsqueeze(1).broadcast_to([128, CONVW, 128]),
                in1=WDW[:, m, :].unsqueeze(2).broadcast_to([128, CONVW, 128]), op=ALU.mult),
                reads=["identb", "WDW"], writes=[("w", slot)])
            for (sid, off, w) in sbs:
                lo, ww = (32, 96) if sid == 0 else (off, w)
                bank = next_bank()
                rd = [("w", slot), ("Bb", m, sid)] + ([("Bb", m, sid - 1)] if sid > 0 else [])
                for k in range(CONVW):
                    S.op("pe", lambda e, slot=slot, m=m, k=k, bank=bank, lo=lo, ww=ww: e.matmul(
                        ps[:, bank, 0:ww], lhsT=dv(slot)[:, k, :], rhs=Bb[:, m, lo - 30 + k:lo - 30 + k + ww],
                        start=(k == 0), stop=(k == CONVW - 1)),
                        reads=rd, writes=[("ps", bank)], inc=(k == CONVW - 1))
                S.op("act", lambda e, m=m, bank=bank, lo=lo, ww=ww: e.activation(
                    out=A[:, m, lo:lo + ww], in_=ps[:, bank, 0:ww], func=AF.Identity, bias=P("cv_bdw", m), scale=1.0),
                    reads=[("ps", bank), "PRM"], writes=[("A", m, sid)])
                if sid == 0:
                    S.op("dve", lambda e, m=m: e.memset(A[:, m, 0:32], 0.0), reads=[("A", m, 0)], writes=[("A", m, 0)])
                else:
                    ls_feed(A, "A", m, sid, off, w)
        ls_flush()
        S.phase("conv LN")
        for (sid, off, w) in sbs:
            ln_stats(A, "A", sid, off, w)
            for m in range(KC):
                t1, t2 = next_tf(), next_tf()
                S.op("dve", lambda e, m=m, t1=t1, off=off, w=w: e.tensor_tensor(
                    out=TF[:, t1, 0:w], in0=A[:, m, off:off + w], in1=ST[:, 0, 0:w], op=ALU.mult),
                    reads=[("A", m, sid), ("st", 0)], writes=[("tf", t1)])
                S.op("dve", lambda e, t1=t1, t2=t2, w=w: e.tensor_tensor(
                    out=TF[:, t2, 0:w], in0=TF[:, t1, 0:w], in1=ST[:, 1, 0:w], op=ALU.add),
                    reads=[("tf", t1), ("st", 1)], writes=[("tf", t2)])
                S.op("act", lambda e, m=m, t2=t2, off=off, w=w: e.activation(
                    out=Bb[:, m, off:off + w], in_=TF[:, t2, 0:w], func=AF.Silu, bias=P("cv_lb", m), scale=P("cv_lg", m)),
                    reads=[("tf", t2), "PRM"], writes=[("Bb", m, sid)])
        S.phase("conv pw2")
        dense_out(conv_w_pw2[0], Bb, "Bb", sbs, "cv_b2", lambda m: Mod(i, 2, m), lead=3)

    def pool(i, sbs_all, sbs_own, first_tile):
        st["layer"] = i
        rmsnorm_mod(sbs_all, lambda m: Mod(i, 6, m), lambda m: Mod(i, 0, m), A, "A")
        allm0 = [("A", m, 0) for m in range(KC)]
        lastsid = sbs_own[-1][0]
        if first_tile:
            S.op("dve", lambda e: e.tensor_scalar(out=A[:, :, 0:HALO], in0=A[:, :, 0:HALO], scalar1=HM, scalar2=None,
                                                  op0=ALU.mult),
                 reads=allm0 + ["MISC"], writes=allm0)
        else:
            S.op("dve", lambda e: e.tensor_copy(out=A[:, :, HALO - 16:HALO], in_=HTAIL[:, :, :]),
                 reads=["HTAIL"], writes=allm0)
        S.op("dve", lambda e: e.tensor_copy(out=HTAIL[:, :, :], in_=A[:, :, TB - 16:TB]),
             reads=[("A", m, lastsid) for m in range(KC)], writes=["HTAIL"])
        S.phase("pool win")

        def win_chunk(m):
            g = m // 4
            win = 2 << g
            for (sid, off, w) in sbs_own:
                rdA = [("A", m, sid), ("A", m, sid - 1)]
                b = off - (win - 1)
                a = 1
                src = None
                while a < win:
                    b2 = b + a
                    ln = off + w - b2
                    pi = next_tf()
                    if src is None:
                        S.op("dve", lambda e, m=m, b=b, a=a, ln=ln, pi=pi: e.tensor_tensor(
                            out=TF[:, pi, 0:ln], in0=A[:, m, b + a:b + a + ln], in1=A[:, m, b:b + ln], op=ALU.add),
                            reads=rdA, writes=[("tf", pi)])
                    else:
                        S.op("dve", lambda e, a=a, ln=ln, pi=pi, src=src: e.tensor_tensor(
                            out=TF[:, pi, 0:ln], in0=TF[:, src, a:a + ln], in1=TF[:, src, 0:ln], op=ALU.add),
                            reads=[("tf", src)], writes=[("tf", pi)])
                    src = pi
                    b = b2
                    a *= 2
                S.op("dve", lambda e, m=m, src=src, off=off, w=w, win=win: e.scalar_tensor_tensor(
                    out=Bb[:, m, off:off + w], in0=TF[:, src, 0:w], scalar=1.0 / win, in1=A[:, m, off:off + w],
                    op0=ALU.mult, op1=ALU.subtract),
                    reads=[("tf", src), ("A", m, sid)], writes=[("Bb", m, sid)])
                if first_tile and sid == 1:
                    ti = next_tf()
                    S.op("dve", lambda e, src=src, ti=ti, g=g: e.tensor_tensor(
                        out=TF[:, ti, 0:16], in0=TF[:, src, 0:16], in1=INVC[:, g, :], op=ALU.mult),
                        reads=[("tf", src), "MISC"], writes=[("tf", ti)])
                    S.op("dve", lambda e, m=m, ti=ti, off=off: e.tensor_tensor(
                        out=Bb[:, m, off:off + 16], in0=TF[:, ti, 0:16], in1=A[:, m, off:off + 16], op=ALU.subtract),
                        reads=[("tf", ti), ("A", m, sid)], writes=[("Bb", m, sid)])
        pwv = pool_w[0].rearrange("g (k p) n -> p (g k) n", p=128)
        S.phase("pool mm")
        slots = [load_piece(pwv[:, :, pc * PW:(pc + 1) * PW]) for pc in range(512 // PW)]
        rs_begin(sbs_own)
        for g in range(4):
            for mm_ in range(4):
                win_chunk(g * 4 + mm_)
            for pc in range(512 // PW):
                slot = slots[pc]
                for cc in range(NCH):
                    m = g * 4 + pc * NCH + cc
                    for (sid, off, w) in sbs_own:
                        bank = mm_group(lambda k, slot=slot, cc=cc, g=g: Wv(slot)[:, g * 4 + k, cc * 128:(cc + 1) * 128],
                                        lambda k, off=off, w=w, g=g: Bb[:, g * 4 + k, off:off + w], 4, w,
                                        lambda k, slot=slot, sid=sid, g=g: [("w", slot), ("Bb", g * 4 + k, sid)])
                        resid_evac(bank, w, P("pl_b", m), Mod(i, 8, m), m, sid, off)

    otoks = []

    xkeys = {}

    def load_x_chunk(t, h, m):
        blo = HALO + h * HW
        bhi = blo + HW
        sids = [1 + h]
        if t == 0 and h == 0:
            blo = 0
            sids = [0, 1]
        gc0 = t * T + blo
        ks = [("X", m, s_) for s_ in sids]
        xkeys.setdefault((t, h), []).extend(ks)
        S.dma("sp", ("x", h), xsems[h],
              lambda e: e.dma_start(out=X[:, m, blo:bhi], in_=xT[m * 128:(m + 1) * 128, gc0:gc0 + bhi - blo]),
              writes=ks)

    def finish_load_group(t, h):
        xtok = (("x", h), S.dma_count[("x", h)])
        for k in xkeys.pop((t, h)):
            S.lastw[k] = xtok

    def final_norm_store(t, sbs_own, load_next):
        S.phase("final")
        while rs["pend"]:
            rs_emit_one()
        for (sid, off, w) in sbs_own:
            bank = rs_take(sid)
            if bank is None:
                bank = stats_sum(lambda m, off=off, w=w: X[:, m, off:off + w], lambda m, sid=sid: [("X", m, sid)], w, True)
            rstd_from(lambda bank=bank, w=w: ps[:, bank, 0:w], w, 0, [("ps", bank)])
            reserved.discard(bank)
            for m in range(KC):
                oi = next_tf()
                S.op("dve", lambda e, m=m, oi=oi, off=off, w=w: e.scalar_tensor_tensor(
                    out=TF[:, oi, 0:w], in0=X[:, m, off:off + w], scalar=P("final_g", m), in1=ST[:, 0, 0:w],
                    op0=ALU.mult, op1=ALU.mult),
                    reads=[("X", m, sid), ("st", 0), "PRM"], writes=[("tf", oi)])
                c0 = t * T + off - HALO
                tok = S.dma("sp", ("o", oi), osem[oi],
                            lambda e, m=m, oi=oi, c0=c0, w=w: e.dma_start(out=yT[m * 128:(m + 1) * 128, c0:c0 + w],
                                                                         in_=TF[:, oi, 0:w]),
                            reads=[("tf", oi)])
                otoks.append(tok)
                if load_next:
                    load_x_chunk(t + 1, sid - 1, m)
            if load_next:
                finish_load_group(t + 1, sid - 1)

    for t in range(NT):
        S.phase(f"=== tile {t}")
        first = (t == 0)
        own = [(1 + h, HALO + h * HW, HW) for h in range(NH)]
        withh = [(0, 0, HALO)] + own
        if first:
            for h in range(NH):
                for m in range(KC):
                    load_x_chunk(0, h, m)
                finish_load_group(0, h)
        for i in range(n_layers):
            kind = i % 3
            mod_drain(i, "m")
            if kind == 0:
                sbs = withh if (first and i == 0) else own
                sgu(i, i // 3, sbs, bg=first)
                mod_drain(i, "f")
                ffn(i, sbs, bg=first)
            elif kind == 1:
                sbs = withh if first else own
                conv(i, sbs, first)
                mod_drain(i, "f")
                ffn(i, sbs, bg=first)
            else:
                pool(i, withh if first else own, own, first)
                mod_drain(i, "f")
                ffn(i, own, bg=first)
        final_norm_store(t, own, t + 1 < NT)

    S.wait_tokens("sp", [(("o", oi), S.dma_count.get(("o", oi), 0)) for oi in range(NTMP) if ("o", oi) in S.dma_count])
    block = es.enter_context(nc.Block())
    S.replay(block)
    es.close()
    return nc, S


def _fm(vec):
    return np.ascontiguousarray(np.asarray(vec, np.float32).reshape(KC, 128).T)


def prepare_inputs(inp, NT, T, ncore=NCORE, seq=SEQ):
    x = np.asarray(inp["x"], np.float32)
    own = NT * T
    per_b = seq // own
    vecs = {}
    for i in range(DEPTH):
        vecs[f"gmix{i}"] = inp["norm_mix_g"][i]
        vecs[f"gffn{i}"] = inp["norm_ffn_g"][i]
        for k in range(6):
            vecs[f"ada{i}_{k}"] = inp["b_ada"][i][k * D:(k + 1) * D]
    for j in range(2):
        vecs[f"sgu_bu{j}"] = inp["sgu_b_in"][j][:D]
        vecs[f"sgu_bv{j}"] = inp["sgu_b_in"][j][D:]
        vecs[f"sgu_vg{j}"] = inp["sgu_v_g"][j]
        vecs[f"sgu_vb{j}"] = inp["sgu_v_b"][j]
        vecs[f"sgu_bo{j}"] = inp["sgu_b_out"][j]
    vecs["cv_ba"] = inp["conv_b_pw1"][0][:D]
    vecs["cv_bb"] = inp["conv_b_pw1"][0][D:]
    vecs["cv_bdw"] = inp["conv_b_dw"][0]
    vecs["cv_lg"] = inp["conv_ln_g"][0]
    vecs["cv_lb"] = inp["conv_ln_b"][0]
    vecs["cv_b2"] = inp["conv_b_pw2"][0]
    vecs["pl_b"] = np.asarray(inp["pool_b"][0]).reshape(D)
    vecs["pl_s"] = inp["pool_scale"][0]
    vecs["final_g"] = inp["final_g"]
    wdw = np.ascontiguousarray(
        np.asarray(inp["conv_w_dw"][0], np.float32).T.reshape(KC, 128, CONVW).transpose(1, 0, 2)).reshape(128, KC * CONVW)
    wsT = np.ascontiguousarray(np.asarray(inp["sgu_w_s"], np.float32).transpose(0, 3, 1, 2)).reshape(2, 128, D)
    bs = np.ascontiguousarray(np.asarray(inp["sgu_b_s"], np.float32)).reshape(2, 1, D)
    cst = np.zeros((3, 128, 128), np.float32)
    cst[0] = np.triu(np.ones((128, 128), np.float32))
    cst[1] = np.eye(128, dtype=np.float32)
    shared = dict(wdw=wdw, wsT=wsT, bs=bs, cst=cst)
    for k in ("w_ada", "sgu_w_in", "sgu_w_out", "conv_w_pw1", "conv_w_pw2", "pool_w", "mlp_w1", "mlp_w2"):
        shared[k] = np.ascontiguousarray(np.asarray(inp[k], np.float32))
    in_maps = []
    for r in range(ncore):
        b, s0 = r // per_b, (r % per_b) * own
        xs = np.zeros((HALO + own, D), np.float32)
        if s0 > 0:
            xs[:HALO] = x[b, s0 - HALO:s0]
        xs[HALO:] = x[b, s0:s0 + own]
        v = dict(vecs)
        v["c"] = inp["c"][b]
        prm = np.stack([_fm(v[n]) for n in PNAMES], axis=1).reshape(128, NV * KC)
        misc = np.zeros((128, 65), np.float32)
        misc[:, 0] = 1.0 if s0 > 0 else 0.0
        for g in range(4):
            win = 2 << g
            for tt in range(16):
                cnt = min(tt + 1, win) if s0 == 0 else win
                misc[:, 1 + g * 16 + tt] = np.float32(1.0) / np.float32(cnt)
        m = dict(shared)
        m["xT"] = np.ascontiguousarray(xs.T)
        m["prm"] = np.ascontiguousarray(prm)
        m["misc"] = misc
        in_maps.append(m)
    return in_maps


_CACHE = {}


def run(inp, NT, n_layers=DEPTH, NH=2, ncore=NCORE, seq=SEQ, trace=False):
    T = NH * 512
    key = (NT, n_layers, NH)
    if key not in _CACHE:
        _CACHE[key] = build_program(NT, n_layers=n_layers, NH=NH)[0]
    nc = _CACHE[key]
    in_maps = prepare_inputs(inp, NT, T, ncore=ncore, seq=seq)
    res = run_bass_kernel_spmd(nc, in_maps, core_ids=list(range(ncore)), **({"trace": True} if trace else {}))
    own = NT * T
    per_b = seq // own
    out = np.empty((BATCH, seq, D), np.float32)
    for r in range(ncore):
        b, s0 = r // per_b, (r % per_b) * own
        out[b, s0:s0 + own, :] = res.results[r]["yT"].T
    return out, res


def kernel(**inputs):
    out, _ = run(inputs, NT=4, n_layers=DEPTH, NH=2)
    return out
```
